# Optimizing a Trainium2 kernel written in Bass

```python
import math
import jax, jax.numpy as jnp
from jax import lax
import numpy as np

D_MODEL = 2048
BATCH = 16
SEQ = 256
DEPTH = 2
DEC_BATCH = 2
DEC_SEQ = 4096
PAST_LEN = 512

GRID_W = 64
D_MIX = D_MODEL
HG_WIDTH = D_MIX // 4
HG_HEADS = 4
HG_DK = HG_WIDTH // HG_HEADS
HG_DV = HG_WIDTH // HG_HEADS
HG_CHUNK = 64
RG_WIDTH = D_MIX // 4
RG_HEADS = 4
RG_BLOCK = RG_WIDTH // RG_HEADS
RG_CONV = 4
RG_C = 8.0
MLA_WIDTH = D_MIX - HG_WIDTH - RG_WIDTH
MLA_HEADS = 8
MLA_V_DIM = MLA_WIDTH // MLA_HEADS
MLA_NOPE_DIM = 128
MLA_ROPE_DIM = 64
MLA_QK_DIM = MLA_NOPE_DIM + MLA_ROPE_DIM
Q_LORA = 512
KV_LORA = 512
ROPE_THETA = 10000.0
Q_BLOCK = 128
IN_COLS = 5 * HG_WIDTH + 2 * RG_WIDTH + Q_LORA + KV_LORA + MLA_ROPE_DIM
N_EXPERTS = 16
CAP_FACTOR = 2
D_FF_EXPERT = 1024
DN_ALPHA = (2.0 * DEPTH) ** 0.25
DN_BETA = (8.0 * DEPTH) ** -0.25
LN_EPS = 1e-5
RMS_EPS = 1e-6

kernel_name = "hybrid_hgrn2_rglru_mla_ec_moe_diffusion_step"


def split_points():
    sizes = [HG_WIDTH] * 5 + [RG_WIDTH] * 2 + [Q_LORA, KV_LORA, MLA_ROPE_DIM]
    return [int(s) for s in np.cumsum(sizes)[:-1]]


def layer_norm(x, g, b):
    xf = x.astype(jnp.float32)
    mu = jnp.mean(xf, -1, keepdims=True)
    var = jnp.mean(jnp.square(xf - mu), -1, keepdims=True)
    return ((xf - mu) * lax.rsqrt(var + LN_EPS) * g + b).astype(x.dtype)


def rms_norm(x, g):
    xf = x.astype(jnp.float32)
    return (xf * lax.rsqrt(jnp.mean(xf * xf, -1, keepdims=True) + RMS_EPS) * g).astype(x.dtype)


def axial_rope_tables(n):
    rows = n // GRID_W
    t = jnp.arange(n)
    row = jnp.repeat(jnp.arange(rows), GRID_W).astype(jnp.float32)
    col = (t % GRID_W).astype(jnp.float32)
    half = MLA_ROPE_DIM // 2
    inv = ROPE_THETA ** (-jnp.arange(0, half, 2, dtype=jnp.float32) / half)
    ar, ac = row[:, None] * inv, col[:, None] * inv
    return jnp.cos(ar), jnp.sin(ar), jnp.cos(ac), jnp.sin(ac)


def rope_1d(x, cos, sin):
    x1, x2 = jnp.split(x, 2, axis=-1)
    return jnp.concatenate([x1 * cos - x2 * sin, x2 * cos + x1 * sin], -1)


def apply_axial_rope(x, tables):
    cr, sr, cc, sc = (a.reshape(a.shape[:1] + (1,) * (x.ndim - 3) + a.shape[1:]) for a in tables)
    xr, xc = jnp.split(x, 2, axis=-1)
    return jnp.concatenate([rope_1d(xr, cr, sr), rope_1d(xc, cc, sc)], -1).astype(x.dtype)


def block_attention(q, k, v):
    b, tq, h, dq = q.shape
    nb = tq // Q_BLOCK
    scale = dq ** -0.5
    qb = q.reshape(b, nb, Q_BLOCK, h, dq).transpose(1, 0, 2, 3, 4)

    def one(qblk):
        s = jnp.einsum('bqhd,bkhd->bhqk', qblk, k, preferred_element_type=jnp.float32) * scale
        p = jax.nn.softmax(s, axis=-1)
        return jnp.einsum('bhqk,bkhd->bqhd', p.astype(v.dtype), v)

    o = lax.map(one, qb)
    return o.transpose(1, 0, 2, 3, 4).reshape(b, tq, h, v.shape[-1])


def hgrn2_chunk_scan(q, k, v, log_f, s0):
    b, t, h, _ = q.shape
    n = t // HG_CHUNK

    def to_chunks(a):
        return a.reshape(b, n, HG_CHUNK, h, a.shape[-1]).transpose(1, 0, 3, 2, 4)

    causal = jnp.tril(jnp.ones((HG_CHUNK, HG_CHUNK), bool))[:, :, None]

    def step(s, inp):
        qc, kc, vc, lf = inp
        cum = jnp.cumsum(lf, axis=-2)
        inter = jnp.einsum('bhcd,bhde->bhce', qc * jnp.exp(cum), s)
        diff = cum[:, :, :, None, :] - cum[:, :, None, :, :]
        decay = jnp.exp(jnp.where(causal, diff, -jnp.inf))
        att = jnp.einsum('bhtd,bhsd,bhtsd->bhts', qc, kc, decay)
        o = inter + jnp.einsum('bhts,bhse->bhte', att, vc)
        last = cum[:, :, -1:, :]
        s_new = jnp.exp(last[:, :, 0, :])[..., None] * s + jnp.einsum('bhsd,bhse->bhde', kc * jnp.exp(last - cum), vc)
        return s_new, o

    s_fin, o = lax.scan(step, s0, (to_chunks(q), to_chunks(k), to_chunks(v), to_chunks(log_f)))
    return o.transpose(1, 0, 3, 2, 4).reshape(b, t, h, v.shape[-1]), s_fin


def hgrn2_direction(q, k, v, log_f, s0, reverse):
    if reverse:
        q, k, v, log_f = (jnp.flip(a, axis=1) for a in (q, k, v, log_f))
    o, s = hgrn2_chunk_scan(q, k, v, log_f, s0)
    if reverse:
        o = jnp.flip(o, axis=1)
    return o, s


def linear_scan(a, bx, h0):
    def comb(l, r):
        return l[0] * r[0], r[0] * l[1] + r[1]
    a_cum, b_cum = lax.associative_scan(comb, (a, bx), axis=1)
    h = a_cum * h0[:, None, :] + b_cum
    return h, h[:, -1]


def rglru_direction(xc, w_r, b_r, w_i, b_i, lam, h0, reverse):
    if reverse:
        xc = jnp.flip(xc, axis=1)
    b, t, w = xc.shape
    xb = xc.reshape(b, t, RG_HEADS, RG_BLOCK)
    r = jax.nn.sigmoid(jnp.einsum('bthi,hij->bthj', xb, w_r.astype(jnp.float32)).reshape(b, t, w) + b_r)
    ig = jax.nn.sigmoid(jnp.einsum('bthi,hij->bthj', xb, w_i.astype(jnp.float32)).reshape(b, t, w) + b_i)
    log_a = -RG_C * r * jax.nn.softplus(-lam.astype(jnp.float32))
    a = jnp.exp(log_a)
    gated = jnp.sqrt(-jnp.expm1(2.0 * log_a)) * (ig * xc)
    h, h_last = linear_scan(a, gated, h0)
    if reverse:
        h = jnp.flip(h, axis=1)
    return h, h_last


def centred_dwconv(x, w, bias):
    y = lax.conv_general_dilated(
        x, w[:, None, :], window_strides=(1,),
        padding=[(RG_CONV // 2, RG_CONV - 1 - RG_CONV // 2)],
        dimension_numbers=('NWC', 'WIO', 'NWC'), feature_group_count=x.shape[-1])
    return y + bias


def mixer(h, p, ctx=None):
    f32 = jnp.float32
    b, t, _ = h.shape
    z = jnp.einsum('btd,de->bte', h, p['w_in'])
    (q_hg, ff_hg, fb_hg, i_hg, g_hg, x_rg, y_rg, cq, ckv_raw, kpe) = jnp.split(z, split_points(), axis=-1)

    qh = jax.nn.silu(q_hg.astype(f32)).reshape(b, t, HG_HEADS, HG_DK)
    vh = i_hg.astype(f32).reshape(b, t, HG_HEADS, HG_DV)
    hg_outs, hg_states = [], []
    for d, f_raw in enumerate((ff_hg, fb_hg)):
        lb = p['hg_lb'][d]
        f = (lb + (1.0 - lb) * jax.nn.sigmoid(f_raw.astype(f32))).reshape(b, t, HG_HEADS, HG_DK)
        s0 = jnp.zeros((b, HG_HEADS, HG_DK, HG_DV), f32) if ctx is None else ctx['hg'][:, d].astype(f32)
        o, s = hgrn2_direction(qh, 1.0 - f, vh, jnp.log(f), s0, reverse=(d == 1))
        hg_outs.append(o)
        hg_states.append(s)
    o_hg = rms_norm(hg_outs[0] + hg_outs[1], p['hg_norm_g'].reshape(HG_HEADS, HG_DV))
    o_hg = o_hg * jax.nn.silu(g_hg.astype(f32)).reshape(b, t, HG_HEADS, HG_DV)
    o_hg = o_hg.reshape(b, t, HG_WIDTH).astype(h.dtype)

    xc = centred_dwconv(x_rg, p['rg_conv_w'], p['rg_conv_b']).astype(f32)
    rg_outs, rg_states = [], []
    for d in range(2):
        h0 = jnp.zeros((b, RG_WIDTH), f32) if ctx is None else ctx['rg'][:, d].astype(f32)
        hd, hl = rglru_direction(xc, p['rg_w_r'][d], p['rg_b_r'][d], p['rg_w_i'][d], p['rg_b_i'][d],
                                 p['rg_lambda'][d], h0, reverse=(d == 1))
        rg_outs.append(hd)
        rg_states.append(hl)
    o_rg = ((rg_outs[0] + rg_outs[1]) * jax.nn.gelu(y_rg.astype(f32))).astype(h.dtype)

    qm = jnp.einsum('btr,re->bte', rms_norm(cq, p['q_norm_g']), p['w_uq']).reshape(b, t, MLA_HEADS, MLA_QK_DIM)
    q_nope, q_pe = qm[..., :MLA_NOPE_DIM], qm[..., MLA_NOPE_DIM:]
    ckv = rms_norm(ckv_raw, p['kv_norm_g'])
    if ctx is None:
        ckv_all, kpe_all = ckv, kpe
    else:
        tabs = axial_rope_tables(t)
        q_pe = apply_axial_rope(q_pe, tabs)
        ckv_all = jnp.concatenate([ckv, ctx['ckv'].astype(ckv.dtype)], axis=1)
        kpe_all = jnp.concatenate([apply_axial_rope(kpe, tabs), ctx['kpe'].astype(kpe.dtype)], axis=1)
    tk = ckv_all.shape[1]
    k_nope = jnp.einsum('btr,re->bte', ckv_all, p['w_uk']).reshape(b, tk, MLA_HEADS, MLA_NOPE_DIM)
    v = jnp.einsum('btr,re->bte', ckv_all, p['w_uv']).reshape(b, tk, MLA_HEADS, MLA_V_DIM)
    k = jnp.concatenate([k_nope, jnp.broadcast_to(kpe_all[:, :, None, :], (b, tk, MLA_HEADS, MLA_ROPE_DIM))], -1)
    o_mla = block_attention(jnp.concatenate([q_nope, q_pe], -1), k, v).reshape(b, t, MLA_WIDTH)

    out = jnp.einsum('bte,ed->btd', jnp.concatenate([o_hg, o_rg, o_mla.astype(h.dtype)], -1), p['w_out'])
    if ctx is None:
        return out, (ckv, kpe, jnp.stack(hg_states, 1), jnp.stack(rg_states, 1))
    return out


def expert_choice_ffn(h, w_router, w_gate, w_up, w_down):
    b, t, d = h.shape
    n = b * t
    cap = CAP_FACTOR * n // N_EXPERTS
    xf = h.reshape(n, d)
    probs = jax.nn.softmax(jnp.einsum('nd,de->ne', xf, w_router).astype(jnp.float32), axis=-1)
    gates, idx = lax.top_k(probs.T, cap)
    xs = jnp.take(xf, idx, axis=0)
    hid = jax.nn.silu(jnp.einsum('ecd,edf->ecf', xs, w_gate)) * jnp.einsum('ecd,edf->ecf', xs, w_up)
    ys = jnp.einsum('ecf,efd->ecd', hid, w_down) * gates[..., None].astype(h.dtype)
    out = jnp.zeros_like(xf).at[idx.reshape(-1)].add(ys.reshape(-1, d))
    return out.reshape(b, t, d)


def trunk_layer(x, mod, p, ctx=None):
    sh1, sc1, g1, sh2, sc2, g2 = jnp.split(mod, 6, axis=-1)
    hm = x * (1.0 + sc1) + sh1
    if ctx is None:
        m, st = mixer(hm, p)
    else:
        m, st = mixer(hm, p, ctx), None
    x = layer_norm(DN_ALPHA * x + g1 * m, p['ln1_g'], p['ln1_b'])
    hf = x * (1.0 + sc2) + sh2
    f = expert_choice_ffn(hf, p['moe_router'], p['moe_w_gate'], p['moe_w_up'], p['moe_w_down'])
    x = layer_norm(DN_ALPHA * x + g2 * f, p['ln2_g'], p['ln2_b'])
    return x, st


def setup_inputs(seed: int = 0) -> dict:
    key = jax.random.key(seed)
    ks = jax.random.split(key, 40)
    f32 = jnp.float32

    def nrm(k, shape, scale=1.0):
        return jax.random.normal(k, shape, f32) * scale

    a8 = jax.random.uniform(ks[15], (DEPTH, 2, RG_WIDTH), f32, minval=0.9, maxval=0.999)
    a0 = a8 ** (1.0 / RG_C)
    return {
        'x_prompt': nrm(ks[0], (BATCH, SEQ, D_MODEL)),
        'x_sample': nrm(ks[1], (DEC_BATCH, DEC_SEQ, D_MODEL)),
        'cache_mla_ckv': nrm(ks[2], (DEC_BATCH, DEPTH, PAST_LEN, KV_LORA)),
        'cache_mla_kpe': nrm(ks[3], (DEC_BATCH, DEPTH, PAST_LEN, MLA_ROPE_DIM)),
        'state_hgrn': nrm(ks[4], (DEC_BATCH, DEPTH, 2, HG_HEADS, HG_DK, HG_DV), 0.5),
        'state_rglru': nrm(ks[5], (DEC_BATCH, DEPTH, 2, RG_WIDTH), 0.5),
        'c': nrm(ks[6], (DEC_BATCH, D_MODEL)),
        'c_ctx': nrm(ks[7], (D_MODEL,)),
        'w_in': nrm(ks[8], (DEPTH, D_MODEL, IN_COLS), D_MODEL ** -0.5),
        'w_out': nrm(ks[9], (DEPTH, D_MIX, D_MODEL), DN_BETA * D_MIX ** -0.5),
        'hg_lb_logits': nrm(ks[10], (DEPTH, 2, HG_WIDTH)),
        'hg_norm_g': 1.0 + nrm(ks[11], (DEPTH, HG_WIDTH), 0.02),
        'rg_conv_w': nrm(ks[12], (DEPTH, RG_CONV, RG_WIDTH), RG_CONV ** -0.5),
        'rg_conv_b': nrm(ks[13], (DEPTH, RG_WIDTH), 0.02),
        'rg_w_r': nrm(ks[14], (DEPTH, 2, RG_HEADS, RG_BLOCK, RG_BLOCK), RG_BLOCK ** -0.5),
        'rg_b_r': nrm(ks[16], (DEPTH, 2, RG_WIDTH), 0.02),
        'rg_w_i': nrm(ks[17], (DEPTH, 2, RG_HEADS, RG_BLOCK, RG_BLOCK), RG_BLOCK ** -0.5),
        'rg_b_i': nrm(ks[18], (DEPTH, 2, RG_WIDTH), 0.02),
        'rg_lambda': jnp.log(a0) - jnp.log1p(-a0),
        'mla_q_norm_g': 1.0 + nrm(ks[19], (DEPTH, Q_LORA), 0.02),
        'mla_kv_norm_g': 1.0 + nrm(ks[20], (DEPTH, KV_LORA), 0.02),
        'mla_w_uq': nrm(ks[21], (DEPTH, Q_LORA, MLA_HEADS * MLA_QK_DIM), Q_LORA ** -0.5),
        'mla_w_uk': nrm(ks[22], (DEPTH, KV_LORA, MLA_HEADS * MLA_NOPE_DIM), KV_LORA ** -0.5),
        'mla_w_uv': nrm(ks[23], (DEPTH, KV_LORA, MLA_HEADS * MLA_V_DIM), DN_BETA * KV_LORA ** -0.5),
        'ada_w': nrm(ks[24], (DEPTH, D_MODEL, 6 * D_MODEL), 0.5 * D_MODEL ** -0.5),
        'ada_b': nrm(ks[25], (DEPTH, 6 * D_MODEL), 0.02),
        'ln1_g': 1.0 + nrm(ks[26], (DEPTH, D_MODEL), 0.02),
        'ln1_b': nrm(ks[27], (DEPTH, D_MODEL), 0.02),
        'ln2_g': 1.0 + nrm(ks[28], (DEPTH, D_MODEL), 0.02),
        'ln2_b': nrm(ks[29], (DEPTH, D_MODEL), 0.02),
        'moe_router': nrm(ks[30], (DEPTH, D_MODEL, N_EXPERTS), D_MODEL ** -0.5),
        'moe_w_gate': nrm(ks[31], (DEPTH, N_EXPERTS, D_MODEL, D_FF_EXPERT), D_MODEL ** -0.5),
        'moe_w_up': nrm(ks[32], (DEPTH, N_EXPERTS, D_MODEL, D_FF_EXPERT), DN_BETA * D_MODEL ** -0.5),
        'moe_w_down': nrm(ks[33], (DEPTH, N_EXPERTS, D_FF_EXPERT, D_MODEL), DN_BETA * D_FF_EXPERT ** -0.5),
    }


def reference(x_prompt, x_sample, cache_mla_ckv, cache_mla_kpe, state_hgrn, state_rglru, c, c_ctx,
              w_in, w_out, hg_lb_logits, hg_norm_g, rg_conv_w, rg_conv_b, rg_w_r, rg_b_r, rg_w_i, rg_b_i,
              rg_lambda, mla_q_norm_g, mla_kv_norm_g, mla_w_uq, mla_w_uk, mla_w_uv, ada_w, ada_b,
              ln1_g, ln1_b, ln2_g, ln2_b, moe_router, moe_w_gate, moe_w_up, moe_w_down):
    lb_cum = jnp.cumsum(jax.nn.softmax(hg_lb_logits.astype(jnp.float32), axis=0), axis=0)
    lb_all = lb_cum - lb_cum[0:1]

    def layer_params(l):
        return {
            'w_in': w_in[l], 'w_out': w_out[l], 'hg_lb': lb_all[l], 'hg_norm_g': hg_norm_g[l],
            'rg_conv_w': rg_conv_w[l], 'rg_conv_b': rg_conv_b[l], 'rg_w_r': rg_w_r[l], 'rg_b_r': rg_b_r[l],
            'rg_w_i': rg_w_i[l], 'rg_b_i': rg_b_i[l], 'rg_lambda': rg_lambda[l],
            'q_norm_g': mla_q_norm_g[l], 'kv_norm_g': mla_kv_norm_g[l],
            'w_uq': mla_w_uq[l], 'w_uk': mla_w_uk[l], 'w_uv': mla_w_uv[l],
            'ln1_g': ln1_g[l], 'ln1_b': ln1_b[l], 'ln2_g': ln2_g[l], 'ln2_b': ln2_b[l],
            'moe_router': moe_router[l], 'moe_w_gate': moe_w_gate[l], 'moe_w_up': moe_w_up[l],
            'moe_w_down': moe_w_down[l],
        }

    y_prompt = x_prompt
    ckvs, kpes, hgs, rgs = [], [], [], []
    for l in range(DEPTH):
        p = layer_params(l)
        mod = (jnp.einsum('d,de->e', jax.nn.silu(c_ctx), ada_w[l]) + ada_b[l])[None, None, :]
        y_prompt, (ckv_l, kpe_l, hg_l, rg_l) = trunk_layer(y_prompt, mod, p)
        ckvs.append(ckv_l)
        kpes.append(kpe_l)
        hgs.append(hg_l)
        rgs.append(rg_l)
    new_mla_ckv = jnp.stack(ckvs, axis=1)
    new_mla_kpe = jnp.stack(kpes, axis=1)
    new_state_hgrn = jnp.stack(hgs, axis=1)
    new_state_rglru = jnp.stack(rgs, axis=1)

    y_sample = x_sample
    for l in range(DEPTH):
        p = layer_params(l)
        mod = (jnp.einsum('bd,de->be', jax.nn.silu(c), ada_w[l]) + ada_b[l])[:, None, :]
        ctx = {'ckv': cache_mla_ckv[:, l], 'kpe': cache_mla_kpe[:, l],
               'hg': state_hgrn[:, l], 'rg': state_rglru[:, l]}
        y_sample, _ = trunk_layer(y_sample, mod, p, ctx)

    return (y_prompt, y_sample, new_mla_ckv, new_mla_kpe, new_state_hgrn, new_state_rglru)
```

```python
import numpy as np
from contextlib import ExitStack
import concourse.bass as bass
import concourse.mybir as mybir
from concourse.bass_utils import run_bass_kernel_spmd

F32 = mybir.dt.float32
BF16 = mybir.dt.bfloat16
AF = mybir.ActivationFunctionType
ALU = mybir.AluOpType
AX = mybir.AxisListType

NDMA_SEMS = 8
L = 2
D = 2048
ZF = 4736
NT = 12288
NTB = 24
NG = 3
ALPHA = (2.0 * L) ** 0.25
LN_EPS = 1e-5
RMS_EPS = 1e-6
ATT_SCALE = 192.0 ** -0.5
NBIS = 36
CH = 32


class Buf:
    __slots__ = ("ap", "w", "r")

    def __init__(self, ap):
        self.ap = ap
        self.w = None
        self.r = {}


class FW:
    def __init__(self, nc, stack):
        self.nc = nc
        self.eng = {"pe": nc.tensor, "act": nc.scalar, "dve": nc.vector, "pool": nc.gpsimd, "sp": nc.sync}
        self.sems = {}
        self.cnt = {}
        for k in self.eng:
            self.sems[k] = stack.enter_context(nc.semaphore("s_" + k))
            self.cnt[k] = 0
        self.dcnt = {}
        self.dnext = {}
        for q in ("sp", "pool", "act"):
            self.dnext[q] = 0
            for i in range(NDMA_SEMS):
                key = ("d", q, i)
                self.sems[key] = stack.enter_context(nc.semaphore("d_%s%d" % (q, i)))
                self.dcnt[key] = 0
        self.ccsem = stack.enter_context(nc.semaphore("cc"))
        self.cccnt = 0
        self.seen = {k: {} for k in self.eng}
        self.rr = 0
        self.dummy = Buf(stack.enter_context(nc.sbuf_tensor('fwdummy', [128, 8], F32))[:])

    def _wait(self, e, key, val):
        if val <= 0:
            return
        if key == e and e == "pe":
            return
        s = self.seen[e]
        if s.get(key, 0) >= val:
            return
        s[key] = val
        self.eng[e].wait_ge(self.sems[key], val)

    def _deps(self, e, reads, writes):
        for b in reads:
            if b.w is not None:
                self._wait(e, b.w[0], b.w[1])
        for b in writes:
            if b.w is not None:
                self._wait(e, b.w[0], b.w[1])
            for k, v in b.r.items():
                self._wait(e, k, v)

    def op(self, e, fn, reads=(), writes=(), signal=True):
        self._deps(e, reads, writes)
        inst = fn(self.eng[e])
        if signal:
            self.cnt[e] += 1
            inst.then_inc(self.sems[e], 1)
            tok = self.cnt[e]
        else:
            tok = self.cnt[e] + 1
        for b in reads:
            if b.r.get(e, 0) < tok:
                b.r[e] = tok
        for b in writes:
            b.w = (e, tok)
            b.r = {}
        return inst

    def dma(self, q, out_ap, in_ap, reads=(), writes=()):
        self._deps(q, reads, writes)
        i = self.dnext[q]
        self.dnext[q] = (i + 1) % NDMA_SEMS
        key = ("d", q, i)
        self._wait(q, key, self.dcnt[key])
        self.dcnt[key] += 16
        inst = self.eng[q].dma_start(out=out_ap, in_=in_ap)
        inst.then_inc(self.sems[key], 16)
        tok = self.dcnt[key]
        for b in reads:
            if b.r.get(key, 0) < tok:
                b.r[key] = tok
        for b in writes:
            b.w = (key, tok)
            b.r = {}
        return inst

    def allgather(self, in_buf, out_buf, groups, light=False):
        e = "pool"
        self._deps(e, [in_buf], [out_buf])
        inst = self.nc.gpsimd.collective_compute("AllGather", ALU.bypass, replica_groups=groups,
                                                 ins=[in_buf.ap.opt()], outs=[out_buf.ap.opt()])
        inst.then_inc(self.ccsem, 1)
        self.cccnt += 1
        self.nc.gpsimd.wait_ge(self.ccsem, self.cccnt)
        out_buf.w = None
        out_buf.r = {}
        in_buf.r = {}
        if not light:
            self.barrier()

    def finish(self, bufs):
        for b in bufs:
            if b.w is not None:
                self._wait("sp", b.w[0], b.w[1])

    def barrier(self):
        for e in self.eng:
            for e2 in self.eng:
                if e2 != e:
                    self._wait(e, e2, self.cnt[e2])
            for key, v in self.dcnt.items():
                self._wait(e, key, v)

    def cast_eng(self):
        self.rr += 1
        return ("act", "dve", "pool")[self.rr % 3]


_UID = [0]


class Pool:
    def __init__(self, nc, stack, shape, dtype, n, name):
        self.bufs = []
        _UID[0] += 1
        name = "%s_u%d" % (name, _UID[0])
        for i in range(n):
            t = stack.enter_context(nc.sbuf_tensor("%s_%d" % (name, i), list(shape), dtype))
            self.bufs.append(Buf(t[:]))
        self.i = 0

    def get(self):
        b = self.bufs[self.i % len(self.bufs)]
        self.i += 1
        return b


_UID = [0]


def sbt(nc, stack, shape, dtype, name):
    _UID[0] += 1
    name = "%s_u%d" % (name, _UID[0])
    return Buf(stack.enter_context(nc.sbuf_tensor(name, list(shape), dtype))[:])


PP = {}
_off = 0
for _n, _w in [("ln1g", L * 16), ("ln1b", L * 16), ("ln2g", L * 16), ("ln2b", L * 16), ("adab", L * 96),
               ("lblog", L * 2 * 4), ("hgg", L * 4), ("cw", L * 4 * 4), ("cb", L * 4), ("br", L * 2 * 4),
               ("bi", L * 2 * 4), ("lam", L * 2 * 4), ("qg", L * 4), ("kvg", L * 4), ("rg0", 2 * L * 2 * 4),
               ("oh", 4), ("cvec", 16 * NG)]:
    PP[_n] = (_off, _w)
    _off += _w
NPP = _off


def _fm(v, nchunk):
    v = np.asarray(v, np.float32)
    lead = v.shape[:-1]
    v = v.reshape(lead + (nchunk, 128))
    v = np.moveaxis(v, -1, 0)
    return np.ascontiguousarray(v).reshape(128, -1)


def build_program(stop=None, debug=False, ncore=8):
    nc = bass.Bass("TRN2", target_bir_lowering=False)
    dt = nc.dram_tensor
    ORDER = ["p1", "hgrn", "rglru", "mla", "p3", "thr", "moe", "p5"]
    need_moe = stop is None or (stop[1] != "ada" and (stop[0], ORDER.index(stop[1])) >= (0, ORDER.index("moe")))
    GATHER = []

    def din(name, shape, dtype=F32):
        return dt(name, list(shape), dtype, kind="ExternalInput").ap()

    def dout(name, shape):
        return dt(name, list(shape), F32, kind="ExternalOutput").ap()

    def dscr(name, shape, dtype=F32):
        return dt(name, list(shape), dtype, kind="Internal").ap()

    xp_in = din("xp", [4096, D])
    xs_in = din("xs", [8192, D])
    cckv = din("cckv", [2, L, 512, 512])
    ckpe = din("ckpe", [2, L, 512, 64])
    sth = din("sth", [2, L, 2, 4, 128, 128])
    pp_in = din("pp", [128, NPP])
    gmat_in = din("gmat", [128, 128])
    rmask_in = din("rmask", [128, 2, 512])
    tmask_in = din("tmask", [CH, 2, CH])
    selall_in = din("selall", [16, 16 * 128])
    ropek = din("ropek", [2, 64, 4096])
    w_in = din("w_in", [L, D, ZF])
    w_out = din("w_out", [L, D, D])
    rg_wr = din("rg_wr", [L, 2, 4, 128, 128])
    rg_wi = din("rg_wi", [L, 2, 4, 128, 128])
    w_uq = din("w_uq", [L, 512, 2048])
    w_uk = din("w_uk", [L, 512, 1024])
    w_uv = din("w_uv", [L, 512, 1024])
    ada_w = din("ada_w", [L, D, 6 * D])
    wrt = din("wrt", [L, D, 16])
    wg = wu = wd = None
    if need_moe:
        wg = din("wg", [L, 16, D, 1024])
        wu = din("wu", [L, 16, D, 1024])
        wd = din("wd", [L, 16, 1024, D])

    yp_o = dout("yp", [4096, D])
    ys_o = dout("ys", [8192, D])
    ckv_o = dout("ckvn", [16, L, 256, 512])
    kpe_o = dout("kpen", [16, L, 256, 64])
    hg_o = dout("hgn", [16, L, 2, 4, 128, 128])
    rg_o = dout("rgn", [16, L, 2, 512])

    xT = Buf(dscr("xT", [D, NT]))
    x1T = Buf(dscr("x1T", [D, NT]))
    fT = Buf(dscr("fT", [D, NT]))
    hfT = Buf(dscr("hfT", [D, NT], BF16))
    zT = Buf(dscr("zT", [ZF, NT]))
    cqT = Buf(dscr("cqT", [512, 4096], BF16))
    omT = Buf(dscr("omT", [D, NT]))
    ag3i = Buf(dscr("ag3i", [16, NT]))
    cci = Buf(dscr("cci", [16, 2]))
    cco = Buf(dscr("cco", [16 * ncore, 2]))
    OUTS = [Buf(a) for a in (yp_o, ys_o, ckv_o, kpe_o, hg_o, rg_o)]
    O_yp, O_ys, O_ckv, O_kpe, O_hg, O_rg = OUTS

    def fmv(ap):
        return ap.rearrange("(c p) t -> p c t", p=128)

    with ExitStack() as top, nc.allow_non_contiguous_dma(reason="small strided parameter / tile loads"):
        fw = FW(nc, top)
        op = fw.op
        ident = sbt(nc, top, [128, 128], F32, "ident")
        identb = sbt(nc, top, [128, 128], BF16, "identb")
        onesb = sbt(nc, top, [128, 128], BF16, "onesb")
        onesf = sbt(nc, top, [128, 128], F32, "onesf")
        mkf = sbt(nc, top, [CH, CH], F32, "mkf")
        mkb = sbt(nc, top, [CH, CH], F32, "mkb")
        rsf = sbt(nc, top, [128, 512], F32, "rsf")
        rsb = sbt(nc, top, [128, 512], F32, "rsb")
        pp = sbt(nc, top, [128, NPP], F32, "pp_sb")
        modT = sbt(nc, top, [128, L * 96 * NG], F32, "modT")
        drv = sbt(nc, top, [128, 64], F32, "drv")
        gmat = sbt(nc, top, [128, 128], F32, "gmat_sb")
        selall = sbt(nc, top, [16, 2048], F32, "selall_sb")
        thr = sbt(nc, top, [128, 8], F32, "thr")
        banks = [Buf(top.enter_context(nc.psum_tensor("bank%d" % i, [128, 512], F32))[:]) for i in range(8)]
        bstate = [0]

        def bank():
            b = banks[bstate[0] % 6]
            bstate[0] += 1
            return b

        def ppc(name, idx):
            o, w = PP[name]
            return pp.ap[:, o + idx:o + idx + 1]

        def modc(l, part, c, g):
            i = ((l * 96) + part * 16 + c) * NG + g
            return modT.ap[:, i:i + 1]

        fw.dma("sp", pp.ap, pp_in, writes=[pp])
        fw.dma("sp", gmat.ap, gmat_in, writes=[gmat])
        fw.dma("sp", selall.ap, selall_in, writes=[selall])
        op("pool", lambda e: e.memset(ident.ap, 0.0), writes=[ident])
        op("pool", lambda e: e.affine_select(out=ident.ap, in_=ident.ap, pattern=[[-1, 128]], compare_op=ALU.not_equal,
                                             fill=1.0, base=0, channel_multiplier=1), reads=[ident], writes=[ident])
        op("dve", lambda e: e.tensor_copy(out=identb.ap, in_=ident.ap), reads=[ident], writes=[identb])
        op("pool", lambda e: e.memset(onesb.ap, 1.0), writes=[onesb])
        op("pool", lambda e: e.memset(onesf.ap, 1.0), writes=[onesf])
        fw.dma("sp", mkf.ap, tmask_in[:, 0, :], writes=[mkf])
        fw.dma("sp", mkb.ap, tmask_in[:, 1, :], writes=[mkb])
        fw.dma("sp", rsf.ap, rmask_in[:, 0, :], writes=[rsf])
        fw.dma("sp", rsb.ap, rmask_in[:, 1, :], writes=[rsb])

        def evac(dst_ap, dst_buf, src_ap, src_buf, eng=None, extra_reads=()):
            e = eng or ("act" if fw.rr % 2 == 0 else "dve")
            fw.rr += 1
            if e == "act":
                op("act", lambda en: en.copy(out=dst_ap, in_=src_ap), reads=[src_buf, *extra_reads], writes=[dst_buf])
            else:
                op(e, lambda en: en.tensor_copy(out=dst_ap, in_=src_ap), reads=[src_buf, *extra_reads], writes=[dst_buf])

        def mmacc(ps_ap, ps_buf, pairs):
            n = len(pairs)
            for i, (lt, rh, bufs) in enumerate(pairs):
                op("pe", lambda e: e.matmul(ps_ap, lhsT=lt, rhs=rh, start=(i == 0), stop=(i == n - 1)),
                   reads=bufs, writes=[ps_buf], signal=(i == n - 1))

        def transpose(ps_ap, ps_buf, in_ap, in_buf, idn):
            op("pe", lambda e: e.transpose(out=ps_ap, in_=in_ap, identity=idn), reads=[in_buf, ident, identb], writes=[ps_buf])

        def wload(ph, dram_ap, shape, stg_pool, bf_pool, q="sp"):
            s = stg_pool.get()
            b = bf_pool.get()
            sl = tuple(slice(0, n) for n in shape)
            fw.dma(q, s.ap[sl], dram_ap, writes=[s])
            ce = fw.cast_eng()
            if ce == "act":
                op("act", lambda e: e.copy(out=b.ap[sl], in_=s.ap[sl]), reads=[s], writes=[b])
            else:
                op(ce, lambda e: e.tensor_copy(out=b.ap[sl], in_=s.ap[sl]), reads=[s], writes=[b])
            return b

        with ExitStack() as ph:
            xin = Pool(nc, ph, [128, D], F32, 2, "xin")
            xtt = Pool(nc, ph, [128, 16, 128], F32, 2, "xtt")
            for tt in range(NT // 128):
                src = xp_in[tt * 128:(tt + 1) * 128, :] if tt < 32 else xs_in[(tt - 32) * 128:(tt - 31) * 128, :]
                xi = xin.get()
                fw.dma("sp", xi.ap, src, writes=[xi])
                xo = xtt.get()
                for g4 in range(4):
                    bk = bank()
                    for j in range(4):
                        c = g4 * 4 + j
                        transpose(bk.ap[:, j * 128:(j + 1) * 128], bk, xi.ap[:, c * 128:(c + 1) * 128], xi, ident.ap)
                    evac(xo.ap[:, g4 * 4:(g4 + 1) * 4, :], xo, bk.ap.rearrange("p (j t) -> p j t", j=4), bk)
                fw.dma("pool", fmv(xT.ap)[:, :, tt * 128:(tt + 1) * 128], xo.ap, reads=[xo], writes=[xT])
            fw.barrier()

        with ExitStack() as ph:
            sil = sbt(nc, ph, [128, 16 * NG], F32, "sil")
            adw = Pool(nc, ph, [128, 16, 512], F32, 2, "adw")
            mrow = sbt(nc, ph, [NG, 6 * D], F32, "mrow")
            o, w = PP["cvec"]
            op("act", lambda e: e.activation(out=sil.ap, in_=pp.ap[:, o:o + 16 * NG], func=AF.Silu), reads=[pp], writes=[sil])
            sil3 = sil.ap.rearrange("p (c g) -> p c g", g=NG)
            for l in range(L):
                for nb in range(24):
                    t = adw.get()
                    fw.dma("sp" if nb % 2 == 0 else "act", t.ap, fmv(ada_w[l])[:, :, nb * 512:(nb + 1) * 512], writes=[t])
                    bk = bank()
                    mmacc(bk.ap[0:NG, :], bk, [(sil3[:, kc, :], t.ap[:, kc, :], [sil, t]) for kc in range(16)])
                    evac(mrow.ap[:, nb * 512:(nb + 1) * 512], mrow, bk.ap[0:NG, :], bk)
                for c in range(96):
                    bk = bank()
                    transpose(bk.ap[:, 0:NG], bk, mrow.ap[0:NG, c * 128:(c + 1) * 128], mrow, ident.ap[0:NG, 0:NG])
                    i0 = (l * 96 + c) * NG
                    ob, _ = PP["adab"]
                    op("dve", lambda e: e.tensor_scalar(out=modT.ap[:, i0:i0 + NG], in0=bk.ap[:, 0:NG],
                                                        scalar1=pp.ap[:, ob + l * 96 + c:ob + l * 96 + c + 1], scalar2=None,
                                                        op0=ALU.add), reads=[bk, pp], writes=[modT])
                for part in (1, 4):
                    i0 = (l * 96 + part * 16) * NG
                    op("dve", lambda e: e.tensor_scalar(out=modT.ap[:, i0:i0 + 16 * NG], in0=modT.ap[:, i0:i0 + 16 * NG], scalar1=1.0,
                                                        scalar2=None, op0=ALU.add), reads=[modT], writes=[modT])
            fw.barrier()

        def phase_p1(l):
            with ExitStack() as ph:
                xs = Pool(nc, ph, [128, 16, 512], F32, 1, "p1x")
                hm = [sbt(nc, ph, [128, 16, 512], BF16, "p1hm%d" % i) for i in range(3)]
                wst = Pool(nc, ph, [128, 16, 256], F32, 2, "p1ws")
                wbf = Pool(nc, ph, [128, 16, 256], BF16, 2, "p1wb")
                zo = Pool(nc, ph, [128, 512], F32, 4, "p1zo")
                for tg in range(NTB // 3):
                    for tb in range(3):
                        x = xs.get()
                        gtb = tg * 3 + tb
                        fw.dma("sp", x.ap, fmv(xT.ap)[:, :, gtb * 512:(gtb + 1) * 512], reads=[xT], writes=[x])
                        g = TBG[gtb]
                        for c in range(16):
                            en = "dve" if c % 2 == 0 else "pool"
                            op(en, lambda e: e.tensor_scalar(out=hm[tb].ap[:, c, :], in0=x.ap[:, c, :], scalar1=modc(l, 1, c, g),
                                                             scalar2=modc(l, 0, c, g), op0=ALU.mult, op1=ALU.add),
                               reads=[x, modT], writes=[hm[tb]])
                    ncg = (ZF + 255) // 256
                    for cg in range(ncg):
                        ncol = min(256, ZF - cg * 256)
                        wb = wload(ph, fmv(w_in[l])[:, :, cg * 256:cg * 256 + ncol], (128, 16, ncol), wst, wbf,
                                   q=("sp" if cg % 2 == 0 else "act"))
                        for mc in range(ncol // 128):
                            f0 = cg * 256 + mc * 128
                            for tb in range(3):
                                bk = bank()
                                mmacc(bk.ap, bk, [(wb.ap[:, kc, mc * 128:(mc + 1) * 128], hm[tb].ap[:, kc, :], [wb, hm[tb]])
                                                  for kc in range(16)])
                                z = zo.get()
                                evac(z.ap, z, bk.ap, bk)
                                gtb = tg * 3 + tb
                                fw.dma("pool", zT.ap[f0:f0 + 128, gtb * 512:(gtb + 1) * 512], z.ap, reads=[z], writes=[zT])
                fw.barrier()

        SEQ_BASE = [s_ * 256 for s_ in range(16)] + [4096, 8192]
        ISP = [True] * 16 + [False, False]
        SIDX = list(range(16)) + [0, 1]
        NSEQ = 18
        TBG = [0] * 8 + [1] * 8 + [2] * 8

        def zsrc(seq, f0, nf, t0, nt):
            return zT.ap[f0:f0 + nf, SEQ_BASE[seq] + t0:SEQ_BASE[seq] + t0 + nt], zT

        zloc = zsrc

        SEQ_T = [256] * 16 + [4096, 4096]
        SEQ_TL = SEQ_T
        SEQ_LO = SEQ_BASE

        def phase_hgrn(l):
            with ExitStack() as ph:
                raw = Pool(nc, ph, [128, 512], F32, 6, "hraw")
                tmpf = Pool(nc, ph, [128, 512], F32, 8, "htmp")
                tbf = Pool(nc, ph, [128, 512], BF16, 2, "hbf")
                QD = [sbt(nc, ph, [128, 512], BF16, "hqd%d" % h) for h in range(4)]
                KD = [sbt(nc, ph, [128, 512], BF16, "hkd%d" % h) for h in range(4)]
                VB = [sbt(nc, ph, [128, 512], BF16, "hvb%d" % h) for h in range(4)]
                E1 = [sbt(nc, ph, [128, 512], F32, "he1%d" % h) for h in range(4)]
                tok = Pool(nc, ph, [CH, 128], BF16, 8, "htok")
                att = Pool(nc, ph, [CH, CH], BF16, 4, "hatt")
                S32 = [sbt(nc, ph, [128, 128], F32, "hS32_%d" % h) for h in range(4)]
                Sbf = [sbt(nc, ph, [128, 128], BF16, "hSbf_%d" % h) for h in range(4)]
                U = Pool(nc, ph, [128, 128], F32, 4, "hU")
                oloc = [sbt(nc, ph, [128, 4096], F32, "holoc%d" % h) for h in range(4)]
                lbt = sbt(nc, ph, [128, 16], F32, "hlb")
                fin = Pool(nc, ph, [128, 512], F32, 3, "hfin")
                o, _ = PP["lblog"]
                if l == 0:
                    op("dve", lambda e: e.memset(lbt.ap[:, 0:8], 0.0), writes=[lbt])
                else:
                    op("dve", lambda e: e.tensor_tensor(out=lbt.ap[:, 0:8], in0=pp.ap[:, o + 8:o + 16], in1=pp.ap[:, o:o + 8],
                                                        op=ALU.subtract), reads=[pp], writes=[lbt])
                    op("act", lambda e: e.activation(out=lbt.ap[:, 0:8], in_=lbt.ap[:, 0:8], func=AF.Sigmoid), reads=[lbt], writes=[lbt])
                op("dve", lambda e: e.tensor_scalar(out=lbt.ap[:, 8:16], in0=lbt.ap[:, 0:8], scalar1=-1.0, scalar2=1.0,
                                                    op0=ALU.mult, op1=ALU.add), reads=[lbt], writes=[lbt])
                for seq in range(NSEQ):
                    T, TL, LO = SEQ_T[seq], SEQ_TL[seq], SEQ_LO[seq]
                    BT = min(512, T)
                    nblk = T // BT
                    nch = BT // CH
                    for h in range(4):
                        op("pool", lambda e: e.memset(oloc[h].ap, 0.0), writes=[oloc[h]])
                    for d in range(2):
                        for h in range(4):
                            if ISP[seq]:
                                op("pool", lambda e: e.memset(S32[h].ap, 0.0), writes=[S32[h]])
                            else:
                                fw.dma("sp", S32[h].ap, sth[SIDX[seq], l, d, h], writes=[S32[h]])
                            op("act", lambda e: e.copy(out=Sbf[h].ap, in_=S32[h].ap), reads=[S32[h]], writes=[Sbf[h]])
                        blks = range(nblk) if d == 0 else range(nblk - 1, -1, -1)
                        for blk in blks:
                            t0 = blk * BT
                            per_h = []
                            for h in range(4):
                                qr = raw.get(); fr = raw.get(); vr = raw.get()
                                for (dst, f0) in ((qr, h * 128), (fr, 512 * (1 + d) + h * 128), (vr, 1536 + h * 128)):
                                    a, sb_ = zsrc(seq, f0, 128, t0, BT)
                                    fw.dma("sp", dst.ap[:, 0:BT], a, reads=[sb_], writes=[dst])
                                qh = tmpf.get(); f = tmpf.get(); lf = tmpf.get(); cum = tmpf.get(); e1 = E1[h]
                                qd = QD[h]; kd = KD[h]; vb = VB[h]
                                S = slice(0, BT)
                                op("act", lambda e: e.activation(out=qh.ap[:, S], in_=qr.ap[:, S], func=AF.Silu), reads=[qr], writes=[qh])
                                op("act", lambda e: e.activation(out=f.ap[:, S], in_=fr.ap[:, S], func=AF.Sigmoid), reads=[fr], writes=[f])
                                op("dve", lambda e: e.tensor_scalar(out=f.ap[:, S], in0=f.ap[:, S], scalar1=lbt.ap[:, 8 + d * 4 + h:9 + d * 4 + h],
                                                                    scalar2=lbt.ap[:, d * 4 + h:d * 4 + h + 1], op0=ALU.mult, op1=ALU.add),
                                   reads=[f, lbt], writes=[f])
                                op("act", lambda e: e.activation(out=lf.ap[:, S], in_=f.ap[:, S], func=AF.Ln), reads=[f], writes=[lf])
                                if d == 0:
                                    op("dve", lambda e: e.tensor_tensor_scan(out=cum.ap[:, S], data0=rsf.ap[:, S], data1=lf.ap[:, S], initial=0.0,
                                                                             op0=ALU.mult, op1=ALU.add), reads=[rsf, lf], writes=[cum])
                                else:
                                    op("dve", lambda e: e.tensor_tensor_scan(out=cum.ap[:, BT - 1::-1] if False else cum.ap[:, S][:, ::-1],
                                                                             data0=rsb.ap[:, S][:, ::-1], data1=lf.ap[:, S][:, ::-1], initial=0.0,
                                                                             op0=ALU.mult, op1=ALU.add), reads=[rsb, lf], writes=[cum])
                                op("act", lambda e: e.activation(out=e1.ap[:, S], in_=cum.ap[:, S], func=AF.Exp), reads=[cum], writes=[e1])
                                op("pool", lambda e: e.tensor_scalar(out=f.ap[:, S], in0=f.ap[:, S], scalar1=-1.0, scalar2=1.0, op0=ALU.mult, op1=ALU.add),
                                   reads=[f], writes=[f])
                                op("act", lambda e: e.activation(out=lf.ap[:, S], in_=cum.ap[:, S], func=AF.Exp, scale=-1.0), reads=[cum], writes=[lf])
                                op("dve", lambda e: e.tensor_tensor(out=qd.ap[:, S], in0=qh.ap[:, S], in1=e1.ap[:, S], op=ALU.mult), reads=[qh, e1], writes=[qd])
                                op("dve", lambda e: e.tensor_tensor(out=kd.ap[:, S], in0=f.ap[:, S], in1=lf.ap[:, S], op=ALU.mult), reads=[f, lf], writes=[kd])
                                op("pool", lambda e: e.tensor_copy(out=vb.ap[:, S], in_=vr.ap[:, S]), reads=[vr], writes=[vb])
                                per_h.append((qd, kd, vb, e1))
                            chs = range(nch) if d == 0 else range(nch - 1, -1, -1)
                            for c in chs:
                                C = slice(c * CH, (c + 1) * CH)
                                for h in range(4):
                                    qd, kd, vb, e1 = per_h[h]
                                    bk = bank()
                                    bkb = bk.ap.bitcast(BF16)
                                    transpose(bkb[0:CH, 0:128], bk, vb.ap[:, C], vb, identb.ap)
                                    transpose(bkb[0:CH, 128:256], bk, kd.ap[:, C], kd, identb.ap)
                                    vt = tok.get(); kt = tok.get()
                                    evac(vt.ap, vt, bkb[0:CH, 0:128], bk, eng="act")
                                    evac(kt.ap, kt, bkb[0:CH, 128:256], bk, eng="dve")
                                    bka = bank()
                                    mmacc(bka.ap[0:CH, 0:CH], bka, [(kd.ap[:, C], qd.ap[:, C], [kd, qd])])
                                    am = att.get()
                                    mk = mkf if d == 0 else mkb
                                    op("dve", lambda e: e.tensor_tensor(out=am.ap, in0=bka.ap[0:CH, 0:CH], in1=mk.ap, op=ALU.mult),
                                       reads=[bka, mk], writes=[am])
                                    bko = bank()
                                    mmacc(bko.ap[:, 0:CH], bko, [(Sbf[h].ap, qd.ap[:, C], [Sbf[h], qd]), (vt.ap, am.ap, [vt, am])])
                                    tg = t0 + c * CH
                                    dst = oloc[h].ap[:, tg:tg + CH]
                                    op("dve", lambda e: e.tensor_tensor(out=dst, in0=bko.ap[:, 0:CH], in1=dst, op=ALU.add),
                                       reads=[bko, oloc[h]], writes=[oloc[h]])
                                    bkt = bank()
                                    mmacc(bkt.ap[:, 0:128], bkt, [(kt.ap, vt.ap, [kt, vt])])
                                    u = U.get()
                                    op("dve", lambda e: e.tensor_tensor(out=u.ap, in0=bkt.ap[:, 0:128], in1=S32[h].ap, op=ALU.add),
                                       reads=[bkt, S32[h]], writes=[u])
                                    ce = c * CH + CH - 1 if d == 0 else c * CH
                                    op("dve", lambda e: e.tensor_scalar(out=S32[h].ap, in0=u.ap, scalar1=e1.ap[:, ce:ce + 1], scalar2=None, op0=ALU.mult),
                                       reads=[u, e1], writes=[S32[h]])
                                    op("act", lambda e: e.copy(out=Sbf[h].ap, in_=S32[h].ap), reads=[S32[h]], writes=[Sbf[h]])
                        if ISP[seq]:
                            for h in range(4):
                                fw.dma("pool", hg_o[SIDX[seq], l, d, h], S32[h].ap, reads=[S32[h]], writes=[O_hg])
                    for h in range(4):
                        for b0 in range(0, TL, 512):
                            n = min(512, TL - b0)
                            S = slice(0, n)
                            sq = tbf.get()
                            op("act", lambda e: e.activation(out=sq.ap[:, S], in_=oloc[h].ap[:, b0:b0 + n], func=AF.Square), reads=[oloc[h]], writes=[sq])
                            bk = bank()
                            mmacc(bk.ap[:, S], bk, [(onesb.ap, sq.ap[:, S], [onesb, sq])])
                            rs = fin.get()
                            op("dve", lambda e: e.tensor_scalar(out=rs.ap[:, S], in0=bk.ap[:, S], scalar1=1.0 / 128, scalar2=RMS_EPS, op0=ALU.mult, op1=ALU.add),
                               reads=[bk], writes=[rs])
                            op("act", lambda e: e.activation(out=rs.ap[:, S], in_=rs.ap[:, S], func=AF.Sqrt), reads=[rs], writes=[rs])
                            op("dve", lambda e: e.reciprocal(out=rs.ap[:, S], in_=rs.ap[:, S]), reads=[rs], writes=[rs])
                            gr = raw.get()
                            a, sb_ = zloc(seq, 2048 + h * 128, 128, b0, n)
                            fw.dma("sp", gr.ap[:, S], a, reads=[sb_], writes=[gr])
                            op("act", lambda e: e.activation(out=gr.ap[:, S], in_=gr.ap[:, S], func=AF.Silu), reads=[gr], writes=[gr])
                            o_ = fin.get()
                            op("dve", lambda e: e.scalar_tensor_tensor(out=o_.ap[:, S], in0=oloc[h].ap[:, b0:b0 + n], scalar=ppc("hgg", l * 4 + h), in1=rs.ap[:, S],
                                                                        op0=ALU.mult, op1=ALU.mult), reads=[oloc[h], rs, pp], writes=[o_])
                            op("dve", lambda e: e.tensor_tensor(out=o_.ap[:, S], in0=o_.ap[:, S], in1=gr.ap[:, S], op=ALU.mult), reads=[o_, gr], writes=[o_])
                            fw.dma("pool", omT.ap[h * 128:(h + 1) * 128, LO + b0:LO + b0 + n], o_.ap[:, S], reads=[o_], writes=[omT])
                fw.barrier()

        def phase_rglru(l):
            with ExitStack() as ph:
                TM = 4096
                xpad = sbt(nc, ph, [128, TM + 4], F32, "rxp")
                xc = sbt(nc, ph, [128, TM], F32, "rxc")
                xcb = sbt(nc, ph, [128, TM], BF16, "rxcb")
                A = sbt(nc, ph, [128, TM], F32, "rA")
                G = sbt(nc, ph, [128, TM], F32, "rG")
                Tm = sbt(nc, ph, [128, TM], F32, "rT")
                H = [sbt(nc, ph, [128, TM], F32, "rH%d" % d) for d in range(2)]
                wst = Pool(nc, ph, [128, 128], F32, 2, "rws")
                wbf = Pool(nc, ph, [128, 128], BF16, 4, "rwb")
                cc = sbt(nc, ph, [128, 24], F32, "rcc")
                o, _ = PP["lam"]
                lam = pp.ap[:, o + l * 8:o + l * 8 + 8]
                op("act", lambda e: e.activation(out=cc.ap[:, 16:24], in_=lam, func=AF.Exp, scale=-1.0), reads=[pp], writes=[cc])
                op("act", lambda e: e.activation(out=cc.ap[:, 16:24], in_=cc.ap[:, 16:24], func=AF.Ln, bias=1.0), reads=[cc], writes=[cc])
                op("dve", lambda e: e.tensor_scalar(out=cc.ap[:, 0:8], in0=cc.ap[:, 16:24], scalar1=-8.0, scalar2=None, op0=ALU.mult), reads=[cc], writes=[cc])
                op("dve", lambda e: e.tensor_scalar(out=cc.ap[:, 8:16], in0=cc.ap[:, 16:24], scalar1=-16.0, scalar2=None, op0=ALU.mult), reads=[cc], writes=[cc])
                for seq in range(NSEQ):
                    T, TL, LO = SEQ_T[seq], SEQ_TL[seq], SEQ_LO[seq]
                    S = slice(0, T)
                    for hb in range(4):
                        op("pool", lambda e: e.memset(xpad.ap[:, 0:2], 0.0), writes=[xpad])
                        op("pool", lambda e: e.memset(xpad.ap[:, T + 2:T + 4], 0.0), writes=[xpad])
                        for t0 in range(0, T, 1024):
                            n = min(1024, T - t0)
                            a, sb_ = zsrc(seq, 2560 + hb * 128, 128, t0, n)
                            fw.dma("sp", xpad.ap[:, 2 + t0:2 + t0 + n], a, reads=[sb_], writes=[xpad])
                        cw = lambda j: ppc("cw", (l * 4 + j) * 4 + hb)
                        op("dve", lambda e: e.tensor_scalar(out=xc.ap[:, S], in0=xpad.ap[:, 0:T], scalar1=cw(0), scalar2=ppc("cb", l * 4 + hb),
                                                            op0=ALU.mult, op1=ALU.add), reads=[xpad, pp], writes=[xc])
                        for j in range(1, 4):
                            op("dve", lambda e: e.scalar_tensor_tensor(out=xc.ap[:, S], in0=xpad.ap[:, j:j + T], scalar=cw(j), in1=xc.ap[:, S],
                                                                        op0=ALU.mult, op1=ALU.add), reads=[xpad, xc, pp], writes=[xc])
                        op("act", lambda e: e.copy(out=xcb.ap[:, S], in_=xc.ap[:, S]), reads=[xc], writes=[xcb])
                        for d in range(2):
                            wr_b = wload(ph, rg_wr[l, d, hb], (128, 128), wst, wbf)
                            wi_b = wload(ph, rg_wi[l, d, hb], (128, 128), wst, wbf)
                            for t0 in range(0, T, 512):
                                n = min(512, T - t0)
                                bk = bank()
                                mmacc(bk.ap[:, 0:n], bk, [(wr_b.ap, xcb.ap[:, t0:t0 + n], [wr_b, xcb])])
                                op("act", lambda e: e.activation(out=A.ap[:, t0:t0 + n], in_=bk.ap[:, 0:n], func=AF.Sigmoid,
                                                                 bias=ppc("br", (l * 2 + d) * 4 + hb)), reads=[bk, pp], writes=[A])
                                bk2 = bank()
                                mmacc(bk2.ap[:, 0:n], bk2, [(wi_b.ap, xcb.ap[:, t0:t0 + n], [wi_b, xcb])])
                                op("act", lambda e: e.activation(out=G.ap[:, t0:t0 + n], in_=bk2.ap[:, 0:n], func=AF.Sigmoid,
                                                                 bias=ppc("bi", (l * 2 + d) * 4 + hb)), reads=[bk2, pp], writes=[G])
                            ci = d * 4 + hb
                            op("act", lambda e: e.activation(out=Tm.ap[:, S], in_=A.ap[:, S], func=AF.Exp, scale=cc.ap[:, 8 + ci:9 + ci]), reads=[A, cc], writes=[Tm])
                            op("act", lambda e: e.activation(out=A.ap[:, S], in_=A.ap[:, S], func=AF.Exp, scale=cc.ap[:, ci:ci + 1]), reads=[A, cc], writes=[A])
                            op("dve", lambda e: e.tensor_scalar(out=Tm.ap[:, S], in0=Tm.ap[:, S], scalar1=-1.0, scalar2=1.0, op0=ALU.mult, op1=ALU.add),
                               reads=[Tm], writes=[Tm])
                            op("dve", lambda e: e.tensor_scalar(out=Tm.ap[:, S], in0=Tm.ap[:, S], scalar1=0.0, scalar2=None, op0=ALU.max), reads=[Tm], writes=[Tm])
                            op("act", lambda e: e.activation(out=Tm.ap[:, S], in_=Tm.ap[:, S], func=AF.Sqrt), reads=[Tm], writes=[Tm])
                            op("pool", lambda e: e.tensor_tensor(out=G.ap[:, S], in0=G.ap[:, S], in1=xc.ap[:, S], op=ALU.mult), reads=[G, xc], writes=[G])
                            op("dve", lambda e: e.tensor_tensor(out=G.ap[:, S], in0=G.ap[:, S], in1=Tm.ap[:, S], op=ALU.mult), reads=[G, Tm], writes=[G])
                            init = 0.0 if ISP[seq] else ppc("rg0", ((SIDX[seq] * L + l) * 2 + d) * 4 + hb)
                            if d == 0:
                                op("dve", lambda e: e.tensor_tensor_scan(out=H[0].ap[:, S], data0=A.ap[:, S], data1=G.ap[:, S], initial=init,
                                                                         op0=ALU.mult, op1=ALU.add), reads=[A, G, pp], writes=[H[0]])
                            else:
                                op("dve", lambda e: e.tensor_tensor_scan(out=H[1].ap[:, S][:, ::-1], data0=A.ap[:, S][:, ::-1], data1=G.ap[:, S][:, ::-1],
                                                                         initial=init, op0=ALU.mult, op1=ALU.add), reads=[A, G, pp], writes=[H[1]])
                            if ISP[seq]:
                                te = T - 1 if d == 0 else 0
                                fw.dma("pool", rg_o[SIDX[seq], l, d, hb * 128:(hb + 1) * 128].rearrange("(p o) -> p o", o=1), H[d].ap[:, te:te + 1],
                                       reads=[H[d]], writes=[O_rg])
                        op("dve", lambda e: e.tensor_tensor(out=Tm.ap[:, S], in0=H[0].ap[:, S], in1=H[1].ap[:, S], op=ALU.add), reads=[H[0], H[1]], writes=[Tm])
                        for t0 in range(0, T, 1024):
                            n = min(1024, T - t0)
                            a_, sb_ = zsrc(seq, 3072 + hb * 128, 128, t0, n)
                            fw.dma("sp", A.ap[:, t0:t0 + n], a_, reads=[sb_], writes=[A])
                        op("act", lambda e: e.activation(out=A.ap[:, S], in_=A.ap[:, S], func=AF.Gelu), reads=[A], writes=[A])
                        op("dve", lambda e: e.tensor_tensor(out=Tm.ap[:, S], in0=Tm.ap[:, S], in1=A.ap[:, S], op=ALU.mult), reads=[Tm, A], writes=[Tm])
                        fw.dma("pool", omT.ap[512 + hb * 128:512 + (hb + 1) * 128, LO:LO + T], Tm.ap[:, S], reads=[Tm], writes=[omT])
                fw.barrier()

        def phase_mla(l):
            with ExitStack() as ph:
                TKM = 4608
                ckvb = sbt(nc, ph, [128, 4, TKM], BF16, "mckv")
                kpeb = sbt(nc, ph, [64, TKM], BF16, "mkpe")
                cqn = sbt(nc, ph, [128, 4, 1024], BF16, "mcqn")
                Kh = sbt(nc, ph, [128, TKM], BF16, "mKh")
                Vh = sbt(nc, ph, [128, 36, 128], BF16, "mVh")
                Qn = sbt(nc, ph, [128, 1024], BF16, "mQn")
                Qp = sbt(nc, ph, [64, 1024], BF16, "mQp")
                wst = Pool(nc, ph, [128, 4, 1024], F32, 1, "mws")
                wk = sbt(nc, ph, [128, 4, 1024], BF16, "mwk")
                wv = sbt(nc, ph, [128, 4, 1024], BF16, "mwv")
                wq = sbt(nc, ph, [128, 4, 2048], BF16, "mwq")
                rawp = Pool(nc, ph, [128, 4, 512], F32, 2, "mraw")
                sqp = Pool(nc, ph, [128, 4, 512], BF16, 2, "msq")
                t5 = Pool(nc, ph, [128, 512], F32, 4, "mt5")
                pt = Pool(nc, ph, [128, 512], BF16, 3, "mpt")
                ctk = Pool(nc, ph, [128, 512], F32, 2, "mctk")
                kraw = Pool(nc, ph, [64, 4, 512], F32, 2, "mkr")
                for (dst, src, ncol) in ((wk, w_uk[l], 1024), (wv, w_uv[l], 1024)):
                    s = wst.get()
                    fw.dma("sp", s.ap, src.rearrange("(c p) n -> p c n", p=128), writes=[s])
                    op("act", lambda e: e.copy(out=dst.ap, in_=s.ap), reads=[s], writes=[dst])
                for hf_ in range(2):
                    s = wst.get()
                    fw.dma("sp", s.ap, w_uq[l].rearrange("(c p) n -> p c n", p=128)[:, :, hf_ * 1024:(hf_ + 1) * 1024], writes=[s])
                    op("dve", lambda e: e.tensor_copy(out=wq.ap[:, :, hf_ * 1024:(hf_ + 1) * 1024], in_=s.ap), reads=[s], writes=[wq])

                def rmsn(src4, n, gname, out_bf_ap, out_bf_buf, out_f32=None):
                    sq = sqp.get()
                    op("act", lambda e: e.activation(out=sq.ap[:, :, 0:n], in_=src4.ap[:, :, 0:n], func=AF.Square), reads=[src4], writes=[sq])
                    bk = bank()
                    mmacc(bk.ap[:, 0:n], bk, [(onesb.ap, sq.ap[:, c, 0:n], [onesb, sq]) for c in range(4)])
                    rs = t5.get()
                    op("dve", lambda e: e.tensor_scalar(out=rs.ap[:, 0:n], in0=bk.ap[:, 0:n], scalar1=1.0 / 512, scalar2=RMS_EPS, op0=ALU.mult, op1=ALU.add),
                       reads=[bk], writes=[rs])
                    op("act", lambda e: e.activation(out=rs.ap[:, 0:n], in_=rs.ap[:, 0:n], func=AF.Sqrt), reads=[rs], writes=[rs])
                    op("dve", lambda e: e.reciprocal(out=rs.ap[:, 0:n], in_=rs.ap[:, 0:n]), reads=[rs], writes=[rs])
                    for c in range(4):
                        if out_f32 is not None:
                            op("dve", lambda e: e.scalar_tensor_tensor(out=out_f32.ap[:, c, 0:n], in0=src4.ap[:, c, 0:n], scalar=ppc(gname, l * 4 + c),
                                                                        in1=rs.ap[:, 0:n], op0=ALU.mult, op1=ALU.mult), reads=[src4, rs, pp], writes=[out_f32])
                            op("pool", lambda e: e.tensor_copy(out=out_bf_ap(c), in_=out_f32.ap[:, c, 0:n]), reads=[out_f32], writes=[out_bf_buf])
                        else:
                            op("dve", lambda e: e.scalar_tensor_tensor(out=out_bf_ap(c), in0=src4.ap[:, c, 0:n], scalar=ppc(gname, l * 4 + c),
                                                                        in1=rs.ap[:, 0:n], op0=ALU.mult, op1=ALU.mult), reads=[src4, rs, pp], writes=[out_bf_buf])

                for seq in range(NSEQ):
                    T, TL, LO = SEQ_T[seq], SEQ_TL[seq], SEQ_LO[seq]
                    TK = T + (0 if ISP[seq] else 512)
                    BT = min(512, T)
                    for t0 in range(0, T, BT):
                        r4 = rawp.get()
                        a, sb_ = zsrc(seq, 4096, 512, t0, BT)
                        fw.dma("sp", r4.ap[:, :, 0:BT], a.rearrange("(c p) t -> p c t", p=128), reads=[sb_], writes=[r4])
                        if ISP[seq]:
                            nf = rawp.get()
                            rmsn(r4, BT, "kvg", lambda c: ckvb.ap[:, c, t0:t0 + BT], ckvb, out_f32=nf)
                            for tt in range(BT // 128):
                                bk = bank()
                                for c in range(4):
                                    transpose(bk.ap[:, c * 128:(c + 1) * 128], bk, nf.ap[:, c, tt * 128:(tt + 1) * 128], nf, ident.ap)
                                ct = ctk.get()
                                evac(ct.ap, ct, bk.ap, bk)
                                fw.dma("pool", ckv_o[SIDX[seq], l, t0 + tt * 128:t0 + (tt + 1) * 128, :], ct.ap, reads=[ct], writes=[O_ckv])
                        else:
                            rmsn(r4, BT, "kvg", lambda c: ckvb.ap[:, c, t0:t0 + BT], ckvb)
                        kr = kraw.get()
                        a, sb_ = zsrc(seq, 4608, 128, t0, BT)
                        fw.dma("sp", kr.ap[:, 0:2, 0:BT], a.rearrange("(c p) t -> p c t", p=64), reads=[sb_], writes=[kr])
                        if ISP[seq]:
                            op("act", lambda e: e.copy(out=kpeb.ap[:, t0:t0 + BT], in_=kr.ap[:, 0, 0:BT]), reads=[kr], writes=[kpeb])
                            for tt in range(BT // 128):
                                bk = bank()
                                transpose(bk.ap[:, 0:64], bk, kr.ap[:, 0, tt * 128:(tt + 1) * 128], kr, ident.ap[0:64, 0:64])
                                ct = ctk.get()
                                evac(ct.ap[:, 0:64], ct, bk.ap[:, 0:64], bk)
                                fw.dma("pool", kpe_o[SIDX[seq], l, t0 + tt * 128:t0 + (tt + 1) * 128, :], ct.ap[:, 0:64], reads=[ct], writes=[O_kpe])
                        else:
                            fw.dma("act", kr.ap[:, 2, 0:BT], ropek[0, :, t0:t0 + BT], writes=[kr])
                            fw.dma("act", kr.ap[:, 3, 0:BT], ropek[1, :, t0:t0 + BT], writes=[kr])
                            op("dve", lambda e: e.tensor_tensor(out=kr.ap[:, 0, 0:BT], in0=kr.ap[:, 0, 0:BT], in1=kr.ap[:, 2, 0:BT], op=ALU.mult), reads=[kr], writes=[kr])
                            op("dve", lambda e: e.tensor_tensor(out=kr.ap[:, 1, 0:BT], in0=kr.ap[:, 1, 0:BT], in1=kr.ap[:, 3, 0:BT], op=ALU.mult), reads=[kr], writes=[kr])
                            op("dve", lambda e: e.tensor_tensor(out=kpeb.ap[:, t0:t0 + BT], in0=kr.ap[:, 0, 0:BT], in1=kr.ap[:, 1, 0:BT], op=ALU.add), reads=[kr], writes=[kpeb])
                    if not ISP[seq]:
                        for tt in range(4):
                            ct = ctk.get()
                            fw.dma("sp", ct.ap, cckv[SIDX[seq], l, tt * 128:(tt + 1) * 128, :], writes=[ct])
                            bk = bank()
                            for c in range(4):
                                transpose(bk.ap[:, c * 128:(c + 1) * 128], bk, ct.ap[:, c * 128:(c + 1) * 128], ct, ident.ap)
                            evac(ckvb.ap[:, :, 4096 + tt * 128:4096 + (tt + 1) * 128], ckvb, bk.ap.rearrange("p (c t) -> p c t", c=4), bk)
                            ck = ctk.get()
                            fw.dma("sp", ck.ap[:, 0:64], ckpe[SIDX[seq], l, tt * 128:(tt + 1) * 128, :], writes=[ck])
                            bk2 = bank()
                            transpose(bk2.ap[0:64, 0:128], bk2, ck.ap[:, 0:64], ck, ident.ap)
                            evac(kpeb.ap[:, 4096 + tt * 128:4096 + (tt + 1) * 128], kpeb, bk2.ap[0:64, 0:128], bk2)
                    for t0 in range(0, T, BT):
                        r4 = rawp.get()
                        a, sb_ = zsrc(seq, 3584, 512, t0, BT)
                        fw.dma("sp", r4.ap[:, :, 0:BT], a.rearrange("(c p) t -> p c t", p=128), reads=[sb_], writes=[r4])
                        cq_t = sqp.get()
                        rmsn(r4, BT, "qg", lambda c: cq_t.ap[:, c, 0:BT], cq_t)
                        fw.dma("pool", cqT.ap.rearrange("(c p) t -> p c t", p=128)[:, :, t0:t0 + BT], cq_t.ap[:, :, 0:BT], reads=[cq_t], writes=[cqT])
                    nkt = TK // 128
                    QG = min(1024, T)
                    for hd in range(8):
                        for t0 in range(0, TK, 512):
                            n = min(512, TK - t0)
                            bk = bank()
                            mmacc(bk.ap[:, 0:n], bk, [(wk.ap[:, c, hd * 128:(hd + 1) * 128], ckvb.ap[:, c, t0:t0 + n], [wk, ckvb]) for c in range(4)])
                            evac(Kh.ap[:, t0:t0 + n], Kh, bk.ap[:, 0:n], bk)
                        for kt in range(0, nkt, 4):
                            nk = min(4, nkt - kt)
                            bk = bank()
                            for j in range(nk):
                                mmacc(bk.ap[:, j * 128:(j + 1) * 128], bk,
                                      [(ckvb.ap[:, c, (kt + j) * 128:(kt + j + 1) * 128], wv.ap[:, c, hd * 128:(hd + 1) * 128], [ckvb, wv]) for c in range(4)])
                            evac(Vh.ap[:, kt:kt + nk, :], Vh, bk.ap[:, 0:nk * 128].rearrange("p (j e) -> p j e", j=nk), bk)
                        for q0 in range(0, T, QG):
                            fw.dma("sp", cqn.ap[:, :, 0:QG], cqT.ap.rearrange("(c p) t -> p c t", p=128)[:, :, q0:q0 + QG], reads=[cqT], writes=[cqn])
                            for t0 in range(0, QG, BT):
                                bk = bank()
                                mmacc(bk.ap[:, 0:BT], bk, [(wq.ap[:, c, hd * 256:hd * 256 + 128], cqn.ap[:, c, t0:t0 + BT], [wq, cqn]) for c in range(4)])
                                evac(Qn.ap[:, t0:t0 + BT], Qn, bk.ap[:, 0:BT], bk)
                                bk = bank()
                                mmacc(bk.ap[0:64, 0:BT], bk, [(wq.ap[:, c, hd * 256 + 128:hd * 256 + 192], cqn.ap[:, c, t0:t0 + BT], [wq, cqn]) for c in range(4)])
                                if ISP[seq]:
                                    evac(Qp.ap[:, t0:t0 + BT], Qp, bk.ap[0:64, 0:BT], bk)
                                else:
                                    bk2 = bank()
                                    mmacc(bk2.ap[0:64, 0:BT], bk2, [(wq.ap[:, c, hd * 256 + 192:hd * 256 + 256], cqn.ap[:, c, t0:t0 + BT], [wq, cqn]) for c in range(4)])
                                    kr = kraw.get()
                                    fw.dma("act", kr.ap[:, 2, 0:BT], ropek[0, :, q0 + t0:q0 + t0 + BT], writes=[kr])
                                    fw.dma("act", kr.ap[:, 3, 0:BT], ropek[1, :, q0 + t0:q0 + t0 + BT], writes=[kr])
                                    op("dve", lambda e: e.tensor_tensor(out=kr.ap[:, 0, 0:BT], in0=bk.ap[0:64, 0:BT], in1=kr.ap[:, 2, 0:BT], op=ALU.mult), reads=[kr, bk], writes=[kr])
                                    op("dve", lambda e: e.tensor_tensor(out=kr.ap[:, 1, 0:BT], in0=bk2.ap[0:64, 0:BT], in1=kr.ap[:, 3, 0:BT], op=ALU.mult), reads=[kr, bk2], writes=[kr])
                                    op("dve", lambda e: e.tensor_tensor(out=Qp.ap[:, t0:t0 + BT], in0=kr.ap[:, 0, 0:BT], in1=kr.ap[:, 1, 0:BT], op=ALU.add), reads=[kr], writes=[Qp])
                            for t0 in range(0, QG, BT):
                                oacc, sacc = banks[6], banks[7]
                                for kt in range(nkt):
                                    bk = bank()
                                    KS = slice(kt * 128, (kt + 1) * 128)
                                    mmacc(bk.ap[:, 0:BT], bk, [(Kh.ap[:, KS], Qn.ap[:, t0:t0 + BT], [Kh, Qn]), (kpeb.ap[:, KS], Qp.ap[:, t0:t0 + BT], [kpeb, Qp])])
                                    p_ = pt.get()
                                    op("act", lambda e: e.activation(out=p_.ap[:, 0:BT], in_=bk.ap[:, 0:BT], func=AF.Exp, scale=ATT_SCALE), reads=[bk], writes=[p_])
                                    op("pe", lambda e: e.matmul(oacc.ap[:, 0:BT], lhsT=Vh.ap[:, kt, :], rhs=p_.ap[:, 0:BT], start=(kt == 0), stop=(kt == nkt - 1)),
                                       reads=[Vh, p_], writes=[oacc], signal=False)
                                    op("pe", lambda e: e.matmul(sacc.ap[:, 0:BT], lhsT=onesb.ap, rhs=p_.ap[:, 0:BT], start=(kt == 0), stop=(kt == nkt - 1)),
                                       reads=[onesb, p_], writes=[sacc, oacc], signal=True)
                                rs = t5.get()
                                op("dve", lambda e: e.reciprocal(out=rs.ap[:, 0:BT], in_=sacc.ap[:, 0:BT]), reads=[sacc], writes=[rs])
                                o_ = t5.get()
                                op("dve", lambda e: e.tensor_tensor(out=o_.ap[:, 0:BT], in0=oacc.ap[:, 0:BT], in1=rs.ap[:, 0:BT], op=ALU.mult), reads=[oacc, rs], writes=[o_])
                                fw.dma("pool", omT.ap[1024 + hd * 128:1024 + (hd + 1) * 128, LO + q0 + t0:LO + q0 + t0 + BT], o_.ap[:, 0:BT], reads=[o_], writes=[omT])
                fw.barrier()

        def layernorm(u, gname, bname, l, tmp_pool, sq_buf, stat_pool, out_cb):
            bk = bank()
            mmacc(bk.ap, bk, [(onesf.ap, u.ap[:, c, :], [onesf, u]) for c in range(16)])
            mean = stat_pool.get()
            op("dve", lambda e: e.tensor_scalar(out=mean.ap, in0=bk.ap, scalar1=1.0 / D, scalar2=None, op0=ALU.mult), reads=[bk], writes=[mean])
            for c in range(16):
                en = "dve" if c % 2 == 0 else "pool"
                op(en, lambda e: e.tensor_tensor(out=u.ap[:, c, :], in0=u.ap[:, c, :], in1=mean.ap, op=ALU.subtract), reads=[u, mean], writes=[u])
            op("act", lambda e: e.activation(out=sq_buf.ap, in_=u.ap, func=AF.Square), reads=[u], writes=[sq_buf])
            bk2 = bank()
            mmacc(bk2.ap, bk2, [(onesb.ap, sq_buf.ap[:, c, :], [onesb, sq_buf]) for c in range(16)])
            rstd = stat_pool.get()
            op("dve", lambda e: e.tensor_scalar(out=rstd.ap, in0=bk2.ap, scalar1=1.0 / D, scalar2=LN_EPS, op0=ALU.mult, op1=ALU.add), reads=[bk2], writes=[rstd])
            op("act", lambda e: e.activation(out=rstd.ap, in_=rstd.ap, func=AF.Sqrt), reads=[rstd], writes=[rstd])
            op("dve", lambda e: e.reciprocal(out=rstd.ap, in_=rstd.ap), reads=[rstd], writes=[rstd])
            for c in range(16):
                en = "dve" if c % 2 == 0 else "pool"
                op(en, lambda e: e.tensor_tensor(out=u.ap[:, c, :], in0=u.ap[:, c, :], in1=rstd.ap, op=ALU.mult), reads=[u, rstd], writes=[u])
                op("dve", lambda e: e.tensor_scalar(out=u.ap[:, c, :], in0=u.ap[:, c, :], scalar1=ppc(gname, l * 16 + c), scalar2=ppc(bname, l * 16 + c),
                                                    op0=ALU.mult, op1=ALU.add), reads=[u, pp], writes=[u])

        def phase_p3(l):
            with ExitStack() as ph:
                omf = Pool(nc, ph, [128, 16, 512], F32, 1, "p3om")
                omb = sbt(nc, ph, [128, 16, 512], BF16, "p3ob")
                xu = sbt(nc, ph, [128, 16, 512], F32, "p3xu")
                sqb = sbt(nc, ph, [128, 16, 512], BF16, "p3sq")
                hfb = sbt(nc, ph, [128, 16, 512], BF16, "p3hb")
                wst = Pool(nc, ph, [128, 16, 256], F32, 2, "p3ws")
                wbf = Pool(nc, ph, [128, 16, 256], BF16, 2, "p3wb")
                stat = Pool(nc, ph, [128, 512], F32, 2, "p3st")
                wr32 = sbt(nc, ph, [128, 16, 16], F32, "p3wr")
                sm = Pool(nc, ph, [128, 16], F32, 4, "p3sm")
                s1 = Pool(nc, ph, [128, 1], F32, 6, "p3s1")
                prT = sbt(nc, ph, [16, 512], F32, "p3pt")
                fw.dma("sp", wr32.ap, wrt[l].rearrange("(c p) e -> p c e", p=128), writes=[wr32])
                for tb in range(NTB):
                    g = TBG[tb]
                    TS = slice(tb * 512, (tb + 1) * 512)
                    om = omf.get()
                    fw.dma("sp", om.ap, fmv(omT.ap)[:, :, TS], reads=[omT], writes=[om])
                    op("act", lambda e: e.copy(out=omb.ap[:, 0:8, :], in_=om.ap[:, 0:8, :]), reads=[om], writes=[omb])
                    op("dve", lambda e: e.tensor_copy(out=omb.ap[:, 8:16, :], in_=om.ap[:, 8:16, :]), reads=[om], writes=[omb])
                    fw.dma("act", xu.ap, fmv(xT.ap)[:, :, TS], reads=[xT], writes=[xu])
                    op("pool", lambda e: e.tensor_scalar(out=xu.ap, in0=xu.ap, scalar1=ALPHA, scalar2=None, op0=ALU.mult), reads=[xu], writes=[xu])
                    for cg in range(8):
                        wb = wload(ph, fmv(w_out[l])[:, :, cg * 256:(cg + 1) * 256], (128, 16, 256), wst, wbf, q=("sp" if cg % 2 == 0 else "act"))
                        for mc in range(2):
                            c = cg * 2 + mc
                            bk = bank()
                            mmacc(bk.ap, bk, [(wb.ap[:, kc, mc * 128:(mc + 1) * 128], omb.ap[:, kc, :], [wb, omb]) for kc in range(16)])
                            op("dve", lambda e: e.scalar_tensor_tensor(out=xu.ap[:, c, :], in0=bk.ap, scalar=modc(l, 2, c, g), in1=xu.ap[:, c, :],
                                                                        op0=ALU.mult, op1=ALU.add), reads=[bk, xu, modT], writes=[xu])
                    layernorm(xu, "ln1g", "ln1b", l, None, sqb, stat, None)
                    fw.dma("pool", fmv(x1T.ap)[:, :, TS], xu.ap, reads=[xu], writes=[x1T])
                    for c in range(16):
                        en = "dve" if c % 2 == 0 else "pool"
                        op(en, lambda e: e.tensor_scalar(out=xu.ap[:, c, :], in0=xu.ap[:, c, :], scalar1=modc(l, 4, c, g), scalar2=modc(l, 3, c, g),
                                                         op0=ALU.mult, op1=ALU.add), reads=[xu, modT, x1T], writes=[xu])
                    op("act", lambda e: e.copy(out=hfb.ap, in_=xu.ap), reads=[xu], writes=[hfb])
                    fw.dma("pool", fmv(hfT.ap)[:, :, TS], hfb.ap, reads=[hfb], writes=[hfT])
                    for tt in range(4):
                        bk = bank()
                        mmacc(bk.ap[:, 0:16], bk, [(xu.ap[:, kc, tt * 128:(tt + 1) * 128], wr32.ap[:, kc, :], [xu, wr32]) for kc in range(16)])
                        mx = s1.get(); ss = s1.get()
                        e_ = sm.get()
                        op("dve", lambda e: e.reduce_max(out=mx.ap, in_=bk.ap[:, 0:16], axis=AX.X), reads=[bk], writes=[mx])
                        op("dve", lambda e: e.tensor_scalar(out=mx.ap, in0=mx.ap, scalar1=-1.0, scalar2=None, op0=ALU.mult), reads=[mx], writes=[mx])
                        op("act", lambda e: e.activation(out=e_.ap, in_=bk.ap[:, 0:16], func=AF.Exp, bias=mx.ap, accum_out=ss.ap), reads=[bk, mx], writes=[e_, ss])
                        op("dve", lambda e: e.reciprocal(out=ss.ap, in_=ss.ap), reads=[ss], writes=[ss])
                        op("dve", lambda e: e.tensor_scalar(out=e_.ap, in0=e_.ap, scalar1=ss.ap, scalar2=None, op0=ALU.mult), reads=[e_, ss], writes=[e_])
                        bk2 = bank()
                        transpose(bk2.ap[0:16, 0:128], bk2, e_.ap, e_, ident.ap)
                        evac(prT.ap[:, tt * 128:(tt + 1) * 128], prT, bk2.ap[0:16, 0:128], bk2)
                    fw.dma("pool", ag3i.ap[:, TS], prT.ap, reads=[prT], writes=[ag3i])
                fw.barrier()

        def phase_thr(l):
            with ExitStack() as ph:
                PB = sbt(nc, ph, [16, NT], F32, "tPB")
                junk = sbt(nc, ph, [16, 8192], F32, "tjk")
                st = sbt(nc, ph, [128, 16], F32, "tst")
                call = sbt(nc, ph, [128, 2], F32, "tcall")
                fw.dma("sp", PB.ap, ag3i.ap, reads=[ag3i], writes=[PB])
                op("dve", lambda e: e.memset(st.ap[:, 0:2], 0.0), writes=[st])
                op("dve", lambda e: e.memset(st.ap[:, 2:4], 1.0), reads=[st], writes=[st])
                op("dve", lambda e: e.memset(st.ap[:, 4:6], 0.5), reads=[st], writes=[st])
                op("dve", lambda e: e.memset(st.ap[:, 6:8], 0.0), reads=[st], writes=[st])
                cap_p, cap_s = 512.0, 1024.0
                op("dve", lambda e: e.memset(st.ap[:, 12:13], cap_p), reads=[st], writes=[st])
                op("dve", lambda e: e.memset(st.ap[:, 13:14], cap_s), reads=[st], writes=[st])
                op("dve", lambda e: e.memset(call.ap, 0.0), writes=[call])
                grp = [slice(0, 4096), slice(4096, NT)]
                for it in range(NBIS):
                    for g in range(2):
                        n = 4096 if g == 0 else 8192
                        op("dve", lambda e: e.tensor_scalar(out=junk.ap[:, 0:n], in0=PB.ap[:, grp[g]], scalar1=st.ap[0:16, 4 + g:5 + g], scalar2=0.0,
                                                            op0=ALU.is_gt, op1=ALU.add, accum_out=st.ap[0:16, 6 + g:7 + g]), reads=[PB, st], writes=[junk, st])
                    fw.dma("pool", cci.ap, st.ap[0:16, 6:8], reads=[st], writes=[cci])
                    fw.dma("pool", call.ap[0:16, :], cci.ap, reads=[cci], writes=[call])
                    bk = bank()
                    mmacc(bk.ap[:, 0:2], bk, [(gmat.ap, call.ap, [gmat, call])])
                    op("dve", lambda e: e.tensor_tensor(out=st.ap[:, 8:10], in0=bk.ap[:, 0:2], in1=st.ap[:, 12:14], op=ALU.is_ge), reads=[bk, st], writes=[st])
                    op("dve", lambda e: e.tensor_tensor(out=st.ap[:, 10:12], in0=st.ap[:, 4:6], in1=st.ap[:, 0:2], op=ALU.subtract), reads=[st], writes=[st])
                    op("dve", lambda e: e.tensor_tensor(out=st.ap[:, 10:12], in0=st.ap[:, 10:12], in1=st.ap[:, 8:10], op=ALU.mult), reads=[st], writes=[st])
                    op("dve", lambda e: e.tensor_tensor(out=st.ap[:, 0:2], in0=st.ap[:, 0:2], in1=st.ap[:, 10:12], op=ALU.add), reads=[st], writes=[st])
                    op("dve", lambda e: e.tensor_tensor(out=st.ap[:, 10:12], in0=st.ap[:, 2:4], in1=st.ap[:, 4:6], op=ALU.subtract), reads=[st], writes=[st])
                    op("dve", lambda e: e.tensor_tensor(out=st.ap[:, 10:12], in0=st.ap[:, 10:12], in1=st.ap[:, 8:10], op=ALU.mult), reads=[st], writes=[st])
                    op("dve", lambda e: e.tensor_tensor(out=st.ap[:, 2:4], in0=st.ap[:, 4:6], in1=st.ap[:, 10:12], op=ALU.add), reads=[st], writes=[st])
                    op("dve", lambda e: e.tensor_tensor(out=st.ap[:, 4:6], in0=st.ap[:, 0:2], in1=st.ap[:, 2:4], op=ALU.add), reads=[st], writes=[st])
                    op("dve", lambda e: e.tensor_scalar(out=st.ap[:, 4:6], in0=st.ap[:, 4:6], scalar1=0.5, scalar2=None, op0=ALU.mult), reads=[st], writes=[st])
                op("dve", lambda e: e.tensor_copy(out=thr.ap[:, 0:2], in_=st.ap[:, 0:2]), reads=[st], writes=[thr])
                fw.barrier()

        def phase_moe(l):
            with ExitStack() as ph:
                prl = sbt(nc, ph, [16, 512], F32, "mprl")
                MG = sbt(nc, ph, [16, 512], F32, "mMG")
                hfb = sbt(nc, ph, [128, 16, 512], BF16, "mhf")
                yacc = sbt(nc, ph, [128, 16, 512], F32, "myacc")
                hid = sbt(nc, ph, [128, 8, 512], BF16, "mhid")
                wdb = sbt(nc, ph, [128, 8, D], BF16, "mwdb")
                wds = Pool(nc, ph, [128, D], F32, 2, "mwds")
                gus = Pool(nc, ph, [128, 8, 512], F32, 2, "mgus")
                gbf = [sbt(nc, ph, [128, 16, 512], BF16, "mgbf%d" % i) for i in range(2)]
                bce = Pool(nc, ph, [128, 512], F32, 2, "mbce")
                t2 = Pool(nc, ph, [128, 512], F32, 4, "mt2")
                for tb in range(NTB):
                    TS = slice(tb * 512, (tb + 1) * 512)
                    g = 0 if tb < 8 else 1
                    fw.dma("sp", prl.ap, ag3i.ap[:, TS], reads=[ag3i], writes=[prl])
                    op("dve", lambda e: e.scalar_tensor_tensor(out=MG.ap, in0=prl.ap, scalar=thr.ap[0:16, g:g + 1], in1=prl.ap,
                                                                op0=ALU.is_gt, op1=ALU.mult), reads=[prl, thr], writes=[MG])
                    fw.dma("sp", hfb.ap, fmv(hfT.ap)[:, :, TS], reads=[hfT], writes=[hfb])
                    for ex in range(16):
                        bc = bce.get()
                        bk = bank()
                        mmacc(bk.ap, bk, [(selall.ap[:, ex * 128:(ex + 1) * 128], MG.ap, [selall, MG])])
                        evac(bc.ap, bc, bk.ap, bk)
                        for fc in range(8):
                            s = wds.get()
                            fw.dma("act", s.ap, wd[l, ex, fc * 128:(fc + 1) * 128, :], writes=[s])
                            ce = fw.cast_eng()
                            if ce == "act":
                                op("act", lambda e: e.copy(out=wdb.ap[:, fc, :], in_=s.ap), reads=[s], writes=[wdb])
                            else:
                                op(ce, lambda e: e.tensor_copy(out=wdb.ap[:, fc, :], in_=s.ap), reads=[s], writes=[wdb])
                            if fc % 4 == 0:
                                hh = fc // 4
                                for wi, wsrc in enumerate((wg, wu)):
                                    for kh in range(2):
                                        st_ = gus.get()
                                        fw.dma("sp", st_.ap, wsrc[l, ex].rearrange("(c p) n -> p c n", p=128)[:, kh * 8:(kh + 1) * 8, hh * 512:(hh + 1) * 512],
                                               writes=[st_])
                                        ce = fw.cast_eng()
                                        dstw = gbf[wi].ap[:, kh * 8:(kh + 1) * 8, :]
                                        if ce == "act":
                                            op("act", lambda e: e.copy(out=dstw, in_=st_.ap), reads=[st_], writes=[gbf[wi]])
                                        else:
                                            op(ce, lambda e: e.tensor_copy(out=dstw, in_=st_.ap), reads=[st_], writes=[gbf[wi]])
                            FS = slice((fc % 4) * 128, (fc % 4 + 1) * 128)
                            bkg = bank()
                            mmacc(bkg.ap, bkg, [(gbf[0].ap[:, kc, FS], hfb.ap[:, kc, :], [gbf[0], hfb]) for kc in range(16)])
                            bku = bank()
                            mmacc(bku.ap, bku, [(gbf[1].ap[:, kc, FS], hfb.ap[:, kc, :], [gbf[1], hfb]) for kc in range(16)])
                            sg = t2.get()
                            op("act", lambda e: e.activation(out=sg.ap, in_=bkg.ap, func=AF.Silu), reads=[bkg], writes=[sg])
                            op("dve", lambda e: e.tensor_tensor(out=sg.ap, in0=bku.ap, in1=sg.ap, op=ALU.mult), reads=[bku, sg], writes=[sg])
                            op("pool", lambda e: e.tensor_tensor(out=hid.ap[:, fc, :], in0=sg.ap, in1=bc.ap, op=ALU.mult), reads=[sg, bc], writes=[hid])
                        for dc in range(16):
                            bk = bank()
                            mmacc(bk.ap, bk, [(wdb.ap[:, fc, dc * 128:(dc + 1) * 128], hid.ap[:, fc, :], [wdb, hid]) for fc in range(8)])
                            if ex == 0:
                                evac(yacc.ap[:, dc, :], yacc, bk.ap, bk, eng="dve")
                            else:
                                op("dve", lambda e: e.tensor_tensor(out=yacc.ap[:, dc, :], in0=bk.ap, in1=yacc.ap[:, dc, :], op=ALU.add), reads=[bk, yacc], writes=[yacc])
                    fw.dma("pool", fmv(fT.ap)[:, :, TS], yacc.ap, reads=[yacc], writes=[fT])
                fw.barrier()

        def phase_p5(l):
            with ExitStack() as ph:
                fu = Pool(nc, ph, [128, 16, 512], F32, 1, "p5f")
                xu = sbt(nc, ph, [128, 16, 512], F32, "p5x")
                sqb = sbt(nc, ph, [128, 16, 512], BF16, "p5sq")
                stat = Pool(nc, ph, [128, 512], F32, 2, "p5st")
                tko = Pool(nc, ph, [128, D], F32, 2, "p5to")
                for tb in range(NTB):
                    g = TBG[tb]
                    TS = slice(tb * 512, (tb + 1) * 512)
                    f = fu.get()
                    fw.dma("sp", f.ap, fmv(fT.ap)[:, :, TS], reads=[fT], writes=[f])
                    fw.dma("act", xu.ap, fmv(x1T.ap)[:, :, TS], reads=[x1T], writes=[xu])
                    op("pool", lambda e: e.tensor_scalar(out=xu.ap, in0=xu.ap, scalar1=ALPHA, scalar2=None, op0=ALU.mult), reads=[xu], writes=[xu])
                    for c in range(16):
                        op("dve", lambda e: e.scalar_tensor_tensor(out=xu.ap[:, c, :], in0=f.ap[:, c, :], scalar=modc(l, 5, c, g), in1=xu.ap[:, c, :],
                                                                    op0=ALU.mult, op1=ALU.add), reads=[f, xu, modT], writes=[xu])
                    layernorm(xu, "ln2g", "ln2b", l, None, sqb, stat, None)
                    if l < L - 1:
                        fw.dma("pool", fmv(xT.ap)[:, :, TS], xu.ap, reads=[xu], writes=[xT])
                    else:
                        for tt in range(4):
                            to = tko.get()
                            for g4 in range(4):
                                bk = bank()
                                for j in range(4):
                                    c = g4 * 4 + j
                                    transpose(bk.ap[:, j * 128:(j + 1) * 128], bk, xu.ap[:, c, tt * 128:(tt + 1) * 128], xu, ident.ap)
                                evac(to.ap[:, g4 * 512:(g4 + 1) * 512], to, bk.ap, bk)
                            r0 = tb * 512 + tt * 128
                            if tb < 8:
                                fw.dma("pool", yp_o[r0:r0 + 128, :], to.ap, reads=[to], writes=[O_yp])
                            else:
                                fw.dma("pool", ys_o[r0 - 4096:r0 - 4096 + 128, :], to.ap, reads=[to], writes=[O_ys])
                fw.barrier()

        dbg = []

        def run_all():
            if stop is not None and stop[1] == 'ada':
                return
            for l in range(L):
                for name, fn in (("p1", phase_p1), ("hgrn", phase_hgrn), ("rglru", phase_rglru), ("mla", phase_mla), ("p3", phase_p3),
                                 ("thr", phase_thr), ("moe", phase_moe), ("p5", phase_p5)):
                    fn(l)
                    if stop is not None and stop == (l, name):
                        return
        run_all()
        if debug and stop is not None:
            with ExitStack() as ph:
                for nm, b in (("xT", xT), ("zT", zT), ("omT", omT), ("x1T", x1T), ("ag3i", ag3i), ("fT", fT)):
                    o_ = Buf(dout("dbg_" + nm, list(b.ap.shape)))
                    fw.dma("sp", o_.ap, b.ap, reads=[b], writes=[o_])
                    OUTS.append(o_)
                for nm, b in (("modT", modT), ("thr", thr)):
                    o_ = Buf(dout("dbg_" + nm, list(b.ap.shape)))
                    fw.dma("sp", o_.ap, b.ap, reads=[b], writes=[o_])
                    OUTS.append(o_)
        fw.barrier()
        fw.finish(OUTS)
    return nc, need_moe


_CACHE = {}


def _rope_tables(pos):
    half = 32
    inv = (10000.0 ** (-np.arange(0, half, 2, dtype=np.float32) / half)).astype(np.float32)
    row = (pos // 64).astype(np.float32)
    col = (pos % 64).astype(np.float32)
    ar = row[None, :] * inv[:, None]
    ac = col[None, :] * inv[:, None]
    cos = np.concatenate([np.cos(ar), np.cos(ar), np.cos(ac), np.cos(ac)], 0)
    sin = np.concatenate([-np.sin(ar), np.sin(ar), -np.sin(ac), np.sin(ac)], 0)
    return np.stack([cos, sin], 0).astype(np.float32)


def kernel(x_prompt, x_sample, cache_mla_ckv, cache_mla_kpe, state_hgrn, state_rglru, c, c_ctx,
           w_in, w_out, hg_lb_logits, hg_norm_g, rg_conv_w, rg_conv_b, rg_w_r, rg_b_r, rg_w_i, rg_b_i,
           rg_lambda, mla_q_norm_g, mla_kv_norm_g, mla_w_uq, mla_w_uk, mla_w_uv, ada_w, ada_b,
           ln1_g, ln1_b, ln2_g, ln2_b, moe_router, moe_w_gate, moe_w_up, moe_w_down):
    f32 = np.float32
    A = lambda v: np.ascontiguousarray(np.asarray(v, f32))
    stop = _CACHE.get("stop")
    debug = _CACHE.get("debug", False)
    ncore = _CACHE.get("ncore", 8)
    key = ("nc", stop, debug, ncore)
    if key not in _CACHE:
        _CACHE[key] = build_program(stop, debug, ncore)
    nc, need_moe = _CACHE[key]
    swap = np.concatenate([np.arange(16, 32), np.arange(0, 16), np.arange(48, 64), np.arange(32, 48)])
    w_in = A(w_in)
    w_in_ext = np.concatenate([w_in, w_in[:, :, 4608 + swap]], axis=2)
    uq = A(mla_w_uq).reshape(L, 512, 8, 192)
    uq_ext = np.concatenate([uq, uq[:, :, :, 128 + swap]], axis=3).reshape(L, 512, 2048)
    gmat = (np.arange(128)[:, None] % 16 == np.arange(128)[None, :] % 16).astype(f32)
    selall = np.zeros((16, 16, 128), f32)
    for e in range(16):
        selall[e, e, :] = 1.0
    selall = selall.reshape(16, 2048)
    ropek = _rope_tables(np.arange(4096))
    rmask = np.ones((128, 2, 512), f32)
    rmask[:, 0, 0::CH] = 0.0
    rmask[:, 1, CH - 1::CH] = 0.0
    ss, tt = np.meshgrid(np.arange(CH), np.arange(CH), indexing="ij")
    tmask = np.stack([(tt >= ss), (tt <= ss)], 1).astype(f32)
    shared = {
        "rmask": rmask, "tmask": np.ascontiguousarray(tmask), "gmat": gmat, "selall": selall, "ropek": ropek, "w_in": np.ascontiguousarray(w_in_ext), "w_out": A(w_out),
        "rg_wr": A(rg_w_r), "rg_wi": A(rg_w_i), "w_uq": np.ascontiguousarray(uq_ext), "w_uk": A(mla_w_uk), "w_uv": A(mla_w_uv),
        "ada_w": A(ada_w), "wrt": A(moe_router),
    }
    if need_moe:
        shared.update({"wg": A(moe_w_gate), "wu": A(moe_w_up), "wd": A(moe_w_down)})
    pp = np.zeros((128, NPP), f32)

    def put(name, arr):
        o, w = PP[name]
        assert arr.shape == (128, w), (name, arr.shape, w)
        pp[:, o:o + w] = arr
    put("ln1g", _fm(ln1_g, 16)); put("ln1b", _fm(ln1_b, 16)); put("ln2g", _fm(ln2_g, 16)); put("ln2b", _fm(ln2_b, 16))
    put("adab", _fm(ada_b, 96))
    put("lblog", _fm(hg_lb_logits, 4)); put("hgg", _fm(hg_norm_g, 4)); put("cw", _fm(rg_conv_w, 4)); put("cb", _fm(rg_conv_b, 4))
    put("br", _fm(rg_b_r, 4)); put("bi", _fm(rg_b_i, 4)); put("lam", _fm(rg_lambda, 4))
    put("qg", _fm(mla_q_norm_g, 4)); put("kvg", _fm(mla_kv_norm_g, 4))
    put("rg0", _fm(state_rglru, 4))
    cv = np.concatenate([A(c_ctx)[None], A(c)], 0)
    put("cvec", np.ascontiguousarray(cv.reshape(NG, 16, 128).transpose(2, 1, 0)).reshape(128, 16 * NG))
    m = dict(shared)
    m.update({
        "xp": A(x_prompt).reshape(4096, D), "xs": A(x_sample).reshape(8192, D),
        "cckv": A(cache_mla_ckv), "ckpe": A(cache_mla_kpe), "sth": A(state_hgrn), "pp": pp,
    })
    in_maps = [m for _ in range(ncore)]
    res = run_bass_kernel_spmd(nc, in_maps, core_ids=list(range(ncore)))
    R = res.results
    _CACHE["last"] = R
    r0 = R[0]
    y_prompt = r0["yp"].reshape(16, 256, D)
    y_sample = r0["ys"].reshape(2, 4096, D)
    ckv_n, kpe_n, hg_n, rg_n = r0["ckvn"], r0["kpen"], r0["hgn"], r0["rgn"]
    return (y_prompt.astype(f32), y_sample.astype(f32), ckv_n.astype(f32), kpe_n.astype(f32), hg_n.astype(f32), rg_n.astype(f32))
```
